# Optimizing a Trainium2 kernel written in Bass

```python
import math
import jax, jax.numpy as jnp
from jax import lax
import numpy as np


D_MODEL = 1024
BATCH = 2
SEQ = 16384
DEPTH = 4

GRID_W = 64
CTX_LEN = 256
N_MIXERS = 4
Q_BLOCK = 128
ROPE_BASE = 10000.0
EPS = 1e-6
ADA_CHUNKS = 6

RET_HEADS = 4
RET_DK = D_MODEL // RET_HEADS
RET_DV = 2 * RET_DK
RET_CHUNK = 128

DIFF_HEADS = 8
DIFF_DH = D_MODEL // (2 * DIFF_HEADS)
DIFF_DV = 2 * DIFF_DH

NA_HEADS = 16
NA_DH = D_MODEL // NA_HEADS
NA_WIN_H = 8
NA_WIN_W = 16

MLA_HEADS = 16
MLA_Q_RANK = 256
MLA_KV_RANK = 128
MLA_NOPE = 64
MLA_ROPE = 32
MLA_V = 64
MLA_QK = MLA_NOPE + MLA_ROPE

FFN_HIDDEN = -(-8 * D_MODEL // (3 * 256)) * 256

kernel_name = 'hybrid_interleaved_diffusion_trunk'


def rms_norm(x):
    xf = x.astype(jnp.float32)
    return (xf * lax.rsqrt(jnp.mean(xf * xf, axis=-1, keepdims=True) + EPS)).astype(x.dtype)


def rms_norm_gain(x, g):
    return rms_norm(x) * g.astype(x.dtype)


def modulate(x, shift, scale):
    return rms_norm(x) * (1.0 + scale) + shift


def swiglu(h, w_in, w_out):
    a, g = jnp.split(h @ w_in, 2, axis=-1)
    return (jax.nn.silu(a) * g) @ w_out


def rope_2d(x):
    n, d = x.shape[1], x.shape[-1]
    half = d // 2
    quarter = half // 2
    pos = jnp.arange(n)
    row = (pos // GRID_W).astype(jnp.float32)
    col = (pos % GRID_W).astype(jnp.float32)
    inv_freq = jnp.power(ROPE_BASE, -jnp.arange(quarter, dtype=jnp.float32) / quarter)
    bshape = (n,) + (1,) * (x.ndim - 3) + (quarter,)

    def rot(t, p):
        ang = (p[:, None] * inv_freq[None, :]).reshape(bshape)
        cos, sin = jnp.cos(ang).astype(t.dtype), jnp.sin(ang).astype(t.dtype)
        t1, t2 = t[..., :quarter], t[..., quarter:]
        return jnp.concatenate([t1 * cos - t2 * sin, t2 * cos + t1 * sin], axis=-1)

    return jnp.concatenate([rot(x[..., :half], row), rot(x[..., half:], col)], axis=-1)


def map_query_blocks(fn, q):
    b, n = q.shape[0], q.shape[1]
    qb = jnp.moveaxis(q.reshape((b, n // Q_BLOCK, Q_BLOCK) + q.shape[2:]), 1, 0)
    out = jnp.moveaxis(lax.map(fn, qb), 0, 1)
    return out.reshape((b, n) + out.shape[3:])


def softmax_attend(q, k, v, scale):
    s = jnp.einsum('bqhd,bkhd->bhqk', q, k).astype(jnp.float32) * scale
    p = jax.nn.softmax(s, axis=-1).astype(v.dtype)
    return jnp.einsum('bhqk,bkhv->bqhv', p, v)


def chunk_retention(q, k, v, log_gamma, s0):
    b, n, h, _ = q.shape
    dv = v.shape[-1]
    nc = n // RET_CHUNK
    f32 = jnp.float32

    def chunks(t):
        return jnp.moveaxis(t.astype(f32).reshape(b, nc, RET_CHUNK, h, t.shape[-1]), 1, 0)

    idx = jnp.arange(RET_CHUNK, dtype=f32)
    rel = idx[:, None] - idx[None, :]
    d_intra = jnp.where(rel >= 0, jnp.exp(jnp.maximum(rel, 0.0) * log_gamma[:, None, None]), 0.0)
    d_cross = jnp.exp((idx + 1.0)[None, :] * log_gamma[:, None]).T[None, :, :, None]
    d_state = jnp.exp((RET_CHUNK - 1.0 - idx)[None, :] * log_gamma[:, None]).T[None, :, :, None]
    d_chunk = jnp.exp(RET_CHUNK * log_gamma)[None, :, None, None]

    def step(s, qkv):
        qc, kc, vc = qkv
        att = jnp.einsum('bihd,bjhd->bhij', qc, kc) * d_intra
        o = jnp.einsum('bhij,bjhv->bihv', att, vc) + jnp.einsum('bihd,bhdv->bihv', qc, s) * d_cross
        s = s * d_chunk + jnp.einsum('bjhd,bjhv->bhdv', kc * d_state, vc)
        return s, o

    _, o = lax.scan(step, s0.astype(f32), (chunks(q), chunks(k), chunks(v)))
    return jnp.moveaxis(o, 0, 1).reshape(b, n, h, dv)


def retention_output(o, g, norm_g, w_out):
    b, n, h, dv = o.shape
    of = o.astype(jnp.float32)
    mu = jnp.mean(of, axis=-1, keepdims=True)
    var = jnp.mean(jnp.square(of - mu), axis=-1, keepdims=True)
    on = ((of - mu) * lax.rsqrt(var + EPS)).astype(g.dtype).reshape(b, n, h * dv) * norm_g
    return (on * jax.nn.silu(g)) @ w_out


def retention_mixer(h_ctx, h_lat, w_in, decay_logit, norm_g, w_out, need_ctx):
    f32 = jnp.float32
    nqk = RET_HEADS * RET_DK
    splits = [nqk, 2 * nqk, 2 * nqk + RET_HEADS * RET_DV]

    def proj(h):
        b, n, _ = h.shape
        q, k, v, g = jnp.split(h @ w_in, splits, axis=-1)
        return (q.reshape(b, n, RET_HEADS, RET_DK),
                k.reshape(b, n, RET_HEADS, RET_DK) * (RET_DK ** -0.5),
                v.reshape(b, n, RET_HEADS, RET_DV), g)

    qc, kc, vc, gc = proj(h_ctx)
    ql, kl, vl, gl = proj(h_lat)
    ql, kl = rope_2d(ql), rope_2d(kl)
    log_gamma = jax.nn.log_sigmoid(decay_logit.astype(f32))
    lf, lb = log_gamma[0], log_gamma[1]
    n_ctx = h_ctx.shape[1]
    pos = jnp.arange(n_ctx, dtype=f32)
    kcf, vcf = kc.astype(f32), vc.astype(f32)
    w_f = jnp.exp((n_ctx - 1.0 - pos)[:, None] * lf[None, :])[None, :, :, None]
    w_b = jnp.exp(pos[:, None] * lb[None, :])[None, :, :, None]
    s_f = jnp.einsum('blhk,blhv->bhkv', kcf * w_f, vcf)
    s_b = jnp.einsum('blhk,blhv->bhkv', kcf * w_b, vcf)
    o_lat = (chunk_retention(ql, kl, vl, lf, s_f)
             + chunk_retention(ql[:, ::-1], kl[:, ::-1], vl[:, ::-1], lb, s_b)[:, ::-1])
    y_lat = retention_output(o_lat, gl, norm_g, w_out)
    if not need_ctx:
        return None, y_lat
    rel = pos[:, None] - pos[None, :]
    d_bi = (jnp.where(rel >= 0, jnp.exp(jnp.maximum(rel, 0.0) * lf[:, None, None]), 0.0)
            + jnp.where(rel <= 0, jnp.exp(jnp.maximum(-rel, 0.0) * lb[:, None, None]), 0.0))
    att = jnp.einsum('bihd,bjhd->bhij', qc.astype(f32), kcf) * d_bi
    o_ctx = jnp.einsum('bhij,bjhv->bihv', att, vcf)
    return retention_output(o_ctx, gc, norm_g, w_out), y_lat


def diff_attend(q, k, v, lam, scale):
    s = jnp.einsum('bqhmd,bkhmd->bhmqk', q, k).astype(jnp.float32) * scale
    p = jax.nn.softmax(s, axis=-1)
    p = p[:, :, 0] - lam * p[:, :, 1]
    return jnp.einsum('bhqk,bkhv->bqhv', p.astype(v.dtype), v)


def diff_attention_mixer(h_ctx, h_lat, w_in, q_norm_g, k_norm_g, lam_vecs, subln_g, w_out,
                         lambda_init, need_ctx):
    nq = DIFF_HEADS * 2 * DIFF_DH

    def proj(h):
        b, n, _ = h.shape
        q, k, v = jnp.split(h @ w_in, [nq, 2 * nq], axis=-1)
        q = rms_norm_gain(q.reshape(b, n, DIFF_HEADS, 2, DIFF_DH), q_norm_g)
        k = rms_norm_gain(k.reshape(b, n, DIFF_HEADS, 2, DIFF_DH), k_norm_g)
        return q, k, v.reshape(b, n, DIFF_HEADS, DIFF_DV)

    qc, kc, vc = proj(h_ctx)
    ql, kl, vl = proj(h_lat)
    ql, kl = rope_2d(ql), rope_2d(kl)
    lv = lam_vecs.astype(jnp.float32)
    lam = jnp.exp(jnp.sum(lv[0] * lv[1])) - jnp.exp(jnp.sum(lv[2] * lv[3])) + lambda_init
    scale = DIFF_DH ** -0.5

    def out(o):
        b, n = o.shape[0], o.shape[1]
        o = rms_norm_gain(o, subln_g) * (1.0 - lambda_init)
        return o.reshape(b, n, DIFF_HEADS * DIFF_DV) @ w_out

    k_all = jnp.concatenate([kc, kl], axis=1)
    v_all = jnp.concatenate([vc, vl], axis=1)
    y_lat = out(map_query_blocks(lambda qb: diff_attend(qb, k_all, v_all, lam, scale), ql))
    if not need_ctx:
        return None, y_lat
    return out(diff_attend(qc, kc, vc, lam, scale)), y_lat


def neighborhood_mixer(h_ctx, h_lat, w_in, q_norm_g, k_norm_g, rel_bias, w_out, need_ctx):
    def proj(h):
        b, n, _ = h.shape
        q, k, v = jnp.split(h @ w_in, 3, axis=-1)
        q = rms_norm_gain(q.reshape(b, n, NA_HEADS, NA_DH), q_norm_g)
        k = rms_norm_gain(k.reshape(b, n, NA_HEADS, NA_DH), k_norm_g)
        return q, k, v.reshape(b, n, NA_HEADS, NA_DH)

    qc, kc, vc = proj(h_ctx)
    ql, kl, vl = proj(h_lat)
    scale = NA_DH ** -0.5
    b, n = ql.shape[0], ql.shape[1]
    rows = n // GRID_W
    wh = min(NA_WIN_H, rows)
    n_loc = wh * NA_WIN_W

    def grid(t):
        return t.reshape(b, rows, GRID_W, NA_HEADS, NA_DH)

    kg, vg = grid(kl), grid(vl)
    col = np.arange(GRID_W)
    col_start = np.clip(col - NA_WIN_W // 2, 0, GRID_W - NA_WIN_W)
    col_idx = col_start[:, None] + np.arange(NA_WIN_W)[None, :]
    dc_idx = col_idx - col[:, None] + (NA_WIN_W - 1)

    def row_block(args):
        r, q_row = args
        r0 = jnp.clip(r - NA_WIN_H // 2, 0, rows - wh)
        k_win = lax.dynamic_slice_in_dim(kg, r0, wh, axis=1)[:, :, col_idx]
        v_win = lax.dynamic_slice_in_dim(vg, r0, wh, axis=1)[:, :, col_idx]
        dr_idx = r0 + jnp.arange(wh) - r + (NA_WIN_H - 1)
        bias = rel_bias[:, dr_idx[:, None, None], dc_idx[None, :, :]].transpose(0, 2, 1, 3)
        s_loc = (jnp.einsum('bqhd,brqwhd->bhqrw', q_row, k_win).astype(jnp.float32) * scale
                 + bias.astype(jnp.float32))
        s_ctx = jnp.einsum('bqhd,bkhd->bhqk', q_row, kc).astype(jnp.float32) * scale
        s = jnp.concatenate([s_loc.reshape(b, NA_HEADS, GRID_W, n_loc), s_ctx], axis=-1)
        p = jax.nn.softmax(s, axis=-1).astype(v_win.dtype)
        p_loc = p[..., :n_loc].reshape(b, NA_HEADS, GRID_W, wh, NA_WIN_W)
        return (jnp.einsum('bhqrw,brqwhv->bqhv', p_loc, v_win)
                + jnp.einsum('bhqk,bkhv->bqhv', p[..., n_loc:], vc))

    o = lax.map(row_block, (jnp.arange(rows), jnp.moveaxis(grid(ql), 1, 0)))
    y_lat = jnp.moveaxis(o, 0, 1).reshape(b, n, NA_HEADS * NA_DH) @ w_out
    if not need_ctx:
        return None, y_lat
    y_ctx = softmax_attend(qc, kc, vc, scale).reshape(b, -1, NA_HEADS * NA_DH) @ w_out
    return y_ctx, y_lat


def split_qk_norm(t, g):
    return jnp.concatenate([rms_norm(t[..., :MLA_NOPE]), rms_norm(t[..., MLA_NOPE:])], axis=-1) * g.astype(t.dtype)


def rotate_rope_part(t):
    return jnp.concatenate([t[..., :MLA_NOPE], rope_2d(t[..., MLA_NOPE:])], axis=-1)


def mla_mixer(h_ctx, h_lat, w_down, q_norm_g, kv_norm_g, w_uq, w_ukv, qk_norm_q, qk_norm_k,
              w_out, need_ctx):
    def keys_values(z, rotate):
        b, n, _ = z.shape
        c_kv = z[..., MLA_Q_RANK:MLA_Q_RANK + MLA_KV_RANK]
        k_rope = z[..., MLA_Q_RANK + MLA_KV_RANK:]
        kv = (rms_norm_gain(c_kv, kv_norm_g) @ w_ukv).reshape(b, n, MLA_HEADS, MLA_NOPE + MLA_V)
        k_rope = jnp.broadcast_to(k_rope[:, :, None, :], (b, n, MLA_HEADS, MLA_ROPE))
        k = split_qk_norm(jnp.concatenate([kv[..., :MLA_NOPE], k_rope], axis=-1), qk_norm_k)
        return (rotate_rope_part(k) if rotate else k), kv[..., MLA_NOPE:]

    def queries(z, rotate):
        b, n, _ = z.shape
        q = (rms_norm_gain(z[..., :MLA_Q_RANK], q_norm_g) @ w_uq).reshape(b, n, MLA_HEADS, MLA_QK)
        q = split_qk_norm(q, qk_norm_q)
        return rotate_rope_part(q) if rotate else q

    scale = MLA_QK ** -0.5
    z_ctx = h_ctx @ w_down
    z_lat = h_lat @ w_down
    kc, vc = keys_values(z_ctx, False)
    kl, vl = keys_values(z_lat, True)
    ql = queries(z_lat, True)
    k_all = jnp.concatenate([kc, kl], axis=1)
    v_all = jnp.concatenate([vc, vl], axis=1)
    b, n = ql.shape[0], ql.shape[1]
    o = map_query_blocks(lambda qb: softmax_attend(qb, k_all, v_all, scale), ql)
    y_lat = o.reshape(b, n, MLA_HEADS * MLA_V) @ w_out
    if not need_ctx:
        return None, y_lat
    o_ctx = softmax_attend(queries(z_ctx, False), kc, vc, scale)
    return o_ctx.reshape(b, -1, MLA_HEADS * MLA_V) @ w_out, y_lat


def setup_inputs(seed: int = 0) -> dict:
    key = jax.random.key(seed)
    keys = iter(jax.random.split(key, 40))
    f32 = jnp.float32

    def normal(shape, scale):
        return jax.random.normal(next(keys), shape, f32) * scale

    def gain(shape):
        return 1.0 + normal(shape, 0.02)

    d = D_MODEL
    n_ret, n_diff, n_na, n_mla = [len(range(m, DEPTH, N_MIXERS)) for m in range(N_MIXERS)]
    gam = 1.0 - 2.0 ** (-5.0 - np.arange(RET_HEADS, dtype=np.float32))
    decay_logit0 = jnp.asarray(np.log(gam / (1.0 - gam)), f32)
    ret_cols = 2 * RET_HEADS * RET_DK + 2 * RET_HEADS * RET_DV
    diff_cols = 2 * DIFF_HEADS * 2 * DIFF_DH + DIFF_HEADS * DIFF_DV
    mla_down_cols = MLA_Q_RANK + MLA_KV_RANK + MLA_ROPE
    return {
        'x': normal((BATCH, SEQ, d), 1.0),
        'c': normal((BATCH, d), 1.0),
        'ctx': normal((BATCH, CTX_LEN, d), 1.0),
        'c_ctx': normal((d,), 1.0),
        'ada_w': normal((DEPTH, d, ADA_CHUNKS * d), 0.5 * d ** -0.5),
        'ada_b': normal((DEPTH, ADA_CHUNKS * d), 0.02),
        'ret_w_in': normal((n_ret, d, ret_cols), d ** -0.5),
        'ret_decay_logit': decay_logit0 + normal((n_ret, 2, RET_HEADS), 0.1),
        'ret_norm_g': gain((n_ret, RET_HEADS * RET_DV)),
        'ret_w_out': normal((n_ret, RET_HEADS * RET_DV, d), (RET_HEADS * RET_DV) ** -0.5),
        'diff_w_in': normal((n_diff, d, diff_cols), d ** -0.5),
        'diff_q_norm_g': gain((n_diff, DIFF_DH)),
        'diff_k_norm_g': gain((n_diff, DIFF_DH)),
        'diff_lambda': normal((n_diff, 4, DIFF_DH), 0.1),
        'diff_subln_g': gain((n_diff, DIFF_DV)),
        'diff_w_out': normal((n_diff, DIFF_HEADS * DIFF_DV, d), (DIFF_HEADS * DIFF_DV) ** -0.5),
        'na_w_in': normal((n_na, d, 3 * NA_HEADS * NA_DH), d ** -0.5),
        'na_q_norm_g': gain((n_na, NA_DH)),
        'na_k_norm_g': gain((n_na, NA_DH)),
        'na_rel_bias': normal((n_na, NA_HEADS, 2 * NA_WIN_H - 1, 2 * NA_WIN_W - 1), 0.1),
        'na_w_out': normal((n_na, NA_HEADS * NA_DH, d), (NA_HEADS * NA_DH) ** -0.5),
        'mla_w_down': normal((n_mla, d, mla_down_cols), d ** -0.5),
        'mla_q_norm_g': gain((n_mla, MLA_Q_RANK)),
        'mla_kv_norm_g': gain((n_mla, MLA_KV_RANK)),
        'mla_w_uq': normal((n_mla, MLA_Q_RANK, MLA_HEADS * MLA_QK), MLA_Q_RANK ** -0.5),
        'mla_w_ukv': normal((n_mla, MLA_KV_RANK, MLA_HEADS * (MLA_NOPE + MLA_V)), MLA_KV_RANK ** -0.5),
        'mla_qk_norm_q': gain((n_mla, MLA_QK)),
        'mla_qk_norm_k': gain((n_mla, MLA_QK)),
        'mla_w_out': normal((n_mla, MLA_HEADS * MLA_V, d), (MLA_HEADS * MLA_V) ** -0.5),
        'ffn_w_in': normal((DEPTH, d, 2 * FFN_HIDDEN), d ** -0.5),
        'ffn_w_out': normal((DEPTH, FFN_HIDDEN, d), FFN_HIDDEN ** -0.5),
    }


def reference(x, c, ctx, c_ctx, ada_w, ada_b,
              ret_w_in, ret_decay_logit, ret_norm_g, ret_w_out,
              diff_w_in, diff_q_norm_g, diff_k_norm_g, diff_lambda, diff_subln_g, diff_w_out,
              na_w_in, na_q_norm_g, na_k_norm_g, na_rel_bias, na_w_out,
              mla_w_down, mla_q_norm_g, mla_kv_norm_g, mla_w_uq, mla_w_ukv,
              mla_qk_norm_q, mla_qk_norm_k, mla_w_out,
              ffn_w_in, ffn_w_out):
    s_lat = jax.nn.silu(c)[:, None, :]
    s_ctx = jax.nn.silu(c_ctx)[None, None, :]
    x_lat, x_ctx = x, ctx
    for i in range(DEPTH):
        need_ctx = i < DEPTH - 1
        mod_l = jnp.split(s_lat @ ada_w[i] + ada_b[i], ADA_CHUNKS, axis=-1)
        mod_c = jnp.split(s_ctx @ ada_w[i] + ada_b[i], ADA_CHUNKS, axis=-1)
        h_lat = modulate(x_lat, mod_l[0], mod_l[1])
        h_ctx = modulate(x_ctx, mod_c[0], mod_c[1])
        kind, j = i % N_MIXERS, i // N_MIXERS
        if kind == 0:
            y_ctx, y_lat = retention_mixer(h_ctx, h_lat, ret_w_in[j], ret_decay_logit[j],
                                           ret_norm_g[j], ret_w_out[j], need_ctx)
        elif kind == 1:
            lambda_init = 0.8 - 0.6 * math.exp(-0.3 * i)
            y_ctx, y_lat = diff_attention_mixer(h_ctx, h_lat, diff_w_in[j], diff_q_norm_g[j],
                                                diff_k_norm_g[j], diff_lambda[j], diff_subln_g[j],
                                                diff_w_out[j], lambda_init, need_ctx)
        elif kind == 2:
            y_ctx, y_lat = neighborhood_mixer(h_ctx, h_lat, na_w_in[j], na_q_norm_g[j],
                                              na_k_norm_g[j], na_rel_bias[j], na_w_out[j], need_ctx)
        else:
            y_ctx, y_lat = mla_mixer(h_ctx, h_lat, mla_w_down[j], mla_q_norm_g[j], mla_kv_norm_g[j],
                                     mla_w_uq[j], mla_w_ukv[j], mla_qk_norm_q[j], mla_qk_norm_k[j],
                                     mla_w_out[j], need_ctx)
        x_lat = x_lat + mod_l[2] * y_lat
        x_lat = x_lat + mod_l[5] * swiglu(modulate(x_lat, mod_l[3], mod_l[4]), ffn_w_in[i], ffn_w_out[i])
        if need_ctx:
            x_ctx = x_ctx + mod_c[2] * y_ctx
            x_ctx = x_ctx + mod_c[5] * swiglu(modulate(x_ctx, mod_c[3], mod_c[4]), ffn_w_in[i], ffn_w_out[i])
    return x_lat
```

```python
import numpy as np
import concourse.bass as bass
import concourse.mybir as mybir
from concourse.bass_utils import run_bass_kernel_spmd

F32 = mybir.dt.float32
BF16 = mybir.dt.bfloat16
AF = mybir.ActivationFunctionType
ALU = mybir.AluOpType
AX = mybir.AxisListType

SAME_ENGINE_SYNC = True
N_DMA_SEMS = 24


class Res:
    __slots__ = ("name", "lw", "rd")

    def __init__(self, name=""):
        self.name = name
        self.lw = None
        self.rd = []


class Prog:
    ENGS = ["pe", "act", "dve", "pool", "sp"]

    def __init__(self, nc):
        self.nc = nc
        self.recs = {e: [] for e in self.ENGS}
        self.dmas = []
        self.ndma = {e: 0 for e in self.ENGS}

    def op(self, eng, fn, reads=(), writes=(), dma=False):
        deps = set()
        for r in reads:
            if r.lw is not None:
                deps.add(r.lw)
        for w in writes:
            if w.lw is not None:
                deps.add(w.lw)
            for t in w.rd:
                deps.add(t)
        idx = len(self.recs[eng])
        if dma:
            k = self.ndma[eng]
            self.ndma[eng] += 1
            tok = ("D", eng, k)
            if k >= N_DMA_SEMS:
                deps.add(("D", eng, k - N_DMA_SEMS))
        else:
            tok = (eng, idx)
        deps.discard(tok)
        rec = {"fn": fn, "deps": deps, "dma": dma, "tok": tok, "sig": False}
        self.recs[eng].append(rec)
        for r in reads:
            r.rd.append(tok)
        for w in writes:
            w.lw = tok
            w.rd = []
        return tok

    def pe(self, fn, reads=(), writes=()):
        return self.op("pe", fn, reads, writes)

    def act(self, fn, reads=(), writes=()):
        return self.op("act", fn, reads, writes)

    def dve(self, fn, reads=(), writes=()):
        return self.op("dve", fn, reads, writes)

    def pool(self, fn, reads=(), writes=()):
        return self.op("pool", fn, reads, writes)

    def dma(self, out, in_, reads=(), writes=(), q="sp", **kw):
        return self.op(q, lambda e: e.dma_start(out=out, in_=in_, **kw), reads, writes, dma=True)

    def emit(self, final_wait_tokens=()):
        nc = self.nc
        recs = self.recs
        for e in self.ENGS:
            for rec in recs[e]:
                for d in rec["deps"]:
                    if d[0] == "D":
                        continue
                    de, di = d
                    if de == e and (not SAME_ENGINE_SYNC or e in ("pe", "sp")):
                        continue
                    recs[de][di]["sig"] = True
        for t in final_wait_tokens:
            if t[0] != "D":
                recs[t[0]][t[1]]["sig"] = True
        cnt = {}
        for e in self.ENGS:
            c = 0
            for i, rec in enumerate(recs[e]):
                if rec["sig"] and not rec["dma"]:
                    c += 1
                cnt[(e, i)] = c
        import contextlib
        with contextlib.ExitStack() as st:
            esem = {e: st.enter_context(nc.semaphore("s_" + e)) for e in self.ENGS}
            dsem = {}
            for e in self.ENGS:
                if self.ndma[e] > 0:
                    dsem[e] = [st.enter_context(nc.semaphore("d_%s_%d" % (e, j)))
                               for j in range(min(N_DMA_SEMS, self.ndma[e]))]
            block = st.enter_context(nc.Block())

            def sem_of(tok):
                if tok[0] == "D":
                    _, q, k = tok
                    return dsem[q][k % N_DMA_SEMS], 16 * (k // N_DMA_SEMS + 1)
                return esem[tok[0]], cnt[tok]

            def make(e):
                def body(eng):
                    water = {}
                    for i, rec in enumerate(recs[e]):
                        need = {}
                        for d in rec["deps"]:
                            if d[0] != "D":
                                if d[0] == e and (not SAME_ENGINE_SYNC or e in ("pe", "sp")):
                                    continue
                            s, v = sem_of(d)
                            key = id(s)
                            if water.get(key, 0) >= v:
                                continue
                            if key not in need or need[key][1] < v:
                                need[key] = (s, v)
                        for key, (s, v) in need.items():
                            eng.wait_ge(s, v)
                            water[key] = v
                        ins = rec["fn"](eng)
                        if rec["dma"]:
                            s, v = sem_of(rec["tok"])
                            ins.then_inc(s, 16)
                        elif rec["sig"]:
                            ins.then_inc(esem[e], 1)
                    if e == "sp":
                        for t in final_wait_tokens:
                            s, v = sem_of(t)
                            eng.wait_ge(s, v)
                return body

            regs = {"pe": block.tensor, "act": block.scalar, "dve": block.vector,
                    "pool": block.gpsimd, "sp": block.sync}
            for e in self.ENGS:
                if recs[e] or e == "sp":
                    regs[e](make(e))

import contextlib
import ml_dtypes

NBF = ml_dtypes.bfloat16


class KB:
    def __init__(self):
        self.nc = bass.Bass("TRN2", target_bir_lowering=False)
        self.st = contextlib.ExitStack()
        self.P = Prog(self.nc)
        self.n = 0

    def din(self, name, shape, dt):
        return self.nc.dram_tensor(name, list(shape), dt, kind="ExternalInput").ap()

    def dout(self, name, shape, dt):
        return self.nc.dram_tensor(name, list(shape), dt, kind="ExternalOutput").ap()

    def sb(self, shape, dt, name=None):
        self.n += 1
        return self.st.enter_context(self.nc.sbuf_tensor(name or "sb%d" % self.n, list(shape), dt))

    def ps(self, name=None, shape=(128, 512), dt=F32):
        self.n += 1
        return self.st.enter_context(self.nc.psum_tensor(name or "ps%d" % self.n, list(shape), dt))

    def finish(self, toks):
        self.P.emit(toks)
        self.st.close()
        return self.nc

    def next_phase(self, toks):
        self.P.emit(toks)
        self.st.close()
        self.nc.all_engine_barrier()
        self.st = contextlib.ExitStack()
        self.P = Prog(self.nc)


def run_spmd(nc, in_maps):
    res = run_bass_kernel_spmd(nc, in_maps, core_ids=list(range(len(in_maps))))
    return res.results


def build_att(NH, DQ, NQ, NK, DV, maps, qtiles, scale, lam_init=None, bias_shape=None):
    kb = KB()
    nc, P = kb.nc, kb.P
    NT = NK // 128
    qT = kb.din("qT", [NH, DQ, NQ], BF16)
    kT = kb.din("kT", [NH, DQ, NK], BF16)
    v = kb.din("v", [NH, 128, NT * DV], BF16)
    oT = kb.dout("oT", [NH, DV, NQ], F32)
    if bias_shape is not None:
        bias = kb.din("bias", bias_shape, F32)
    if maps == 2:
        lam = kb.din("lam", [4, 64], F32)
    NB = 2
    kt_sb = [kb.sb([DQ, NK], BF16) for _ in range(NB)]
    v_sb = [kb.sb([128, NT * DV], BF16) for _ in range(NB)]
    r_kv = [Res() for _ in range(NB)]
    q_sb = [kb.sb([DQ, 512], BF16) for _ in range(2)]
    r_q = [Res() for _ in range(2)]
    ones = kb.sb([128, 128], BF16)
    r_ones = Res()
    P.pool(lambda e: e.memset(ones[:], 1.0), [], [r_ones])
    NPB = 4
    p_sb = [[kb.sb([128, 512], BF16) for _ in range(NPB)] for _ in range(maps)]
    r_p = [[Res() for _ in range(NPB)] for _ in range(maps)]
    s_ps = [[kb.ps() for _ in range(2)] for _ in range(maps)]
    r_s = [[Res() for _ in range(2)] for _ in range(maps)]
    o_ps = [kb.ps() for _ in range(maps)]
    l_ps = [kb.ps() for _ in range(maps)]
    r_o = [Res() for _ in range(maps)]
    r_l = [Res() for _ in range(maps)]
    rec_sb = [kb.sb([128, 512], F32) for _ in range(maps)]
    r_rec = [Res() for _ in range(maps)]
    t_sb = [kb.sb([128, 512], F32) for _ in range(maps)]
    r_t = [Res() for _ in range(maps)]
    out_sb = [kb.sb([128, 512], F32) for _ in range(2)]
    r_out = [Res() for _ in range(2)]
    if bias_shape is not None:
        b_sb = [kb.sb([128, 512], F32) for _ in range(3)]
        r_b = [Res() for _ in range(3)]
        sb_sb = [kb.sb([128, 512], F32) for _ in range(2)]
        r_sb = [Res() for _ in range(2)]
    if maps == 2:
        lv = kb.sb([128, 256], F32)
        r_lv = Res()
        neglam = kb.sb([128, 1], F32)
        P.dma(lv[:], lam.rearrange("a b -> (a b)").partition_broadcast(128), writes=[r_lv])
        pr = kb.sb([128, 128], F32)
        sm = kb.sb([128, 2], F32)
        P.dve(lambda e: e.tensor_tensor(out=pr[:, 0:64], in0=lv[:, 0:64], in1=lv[:, 64:128], op=ALU.mult), [r_lv], [r_lv])
        P.dve(lambda e: e.tensor_tensor(out=pr[:, 64:128], in0=lv[:, 128:192], in1=lv[:, 192:256], op=ALU.mult), [r_lv], [r_lv])
        P.dve(lambda e: e.reduce_sum(out=sm[:, 0:1], in_=pr[:, 0:64], axis=AX.X), [r_lv], [r_lv])
        P.dve(lambda e: e.reduce_sum(out=sm[:, 1:2], in_=pr[:, 64:128], axis=AX.X), [r_lv], [r_lv])
        P.act(lambda e: e.activation(out=sm[:], in_=sm[:], func=AF.Exp), [r_lv], [r_lv])
        P.dve(lambda e: e.scalar_tensor_tensor(out=neglam[:], in0=sm[:, 1:2], scalar=-float(lam_init),
                                               in1=sm[:, 0:1], op0=ALU.add, op1=ALU.subtract), [r_lv], [r_lv])
    out_toks = []
    cnt = {"q": 0, "s": 0, "p": 0, "o": 0, "b": 0, "sb": 0}
    rows = [(0, DQ)] if maps == 1 else [(0, 64), (64, 128)]
    for h in range(NH):
        hb = h % NB
        P.dma(kt_sb[hb][:], kT[h], writes=[r_kv[hb]])
        P.dma(v_sb[hb][:], v[h], writes=[r_kv[hb]], reads=[])
        for (q0, nq, ktl) in qtiles:
            qb = cnt["q"] % 2
            cnt["q"] += 1
            P.dma(q_sb[qb][:, :nq], qT[h, :, q0:q0 + nq], writes=[r_q[qb]])
            nkt = len(ktl)
            slots = []
            for ki in range(nkt):
                sl = []
                for m in range(maps):
                    sl.append((cnt["s"] % 2, cnt["p"] % NPB))
                cnt["s"] += 1
                cnt["p"] += 1
                slots.append(sl)

            def emit_qk(ki):
                kt, bi = ktl[ki]
                for m in range(maps):
                    r0, r1 = rows[m]
                    sbi, pbi = slots[ki][m]
                    sp_ = s_ps[m][sbi]
                    P.pe(lambda e, sp_=sp_, hb=hb, qb=qb, r0=r0, r1=r1, kt=kt, nq=nq: e.matmul(
                        sp_[:, :nq], lhsT=kt_sb[hb][r0:r1, kt * 128:(kt + 1) * 128],
                        rhs=q_sb[qb][r0:r1, :nq], start=True, stop=True),
                        [r_kv[hb], r_q[qb]], [r_s[m][sbi]])

            def emit_rest(ki):
                kt, bi = ktl[ki]
                for m in range(maps):
                    sbi, pbi = slots[ki][m]
                    sp_, pp_ = s_ps[m][sbi], p_sb[m][pbi]
                    if bi is not None:
                        bb = cnt["b"] % 3
                        cnt["b"] += 1
                        s2 = cnt["sb"] % 2
                        cnt["sb"] += 1
                        P.dma(b_sb[bb][:, :nq], bias[bi, h], writes=[r_b[bb]])
                        P.dve(lambda e, sp_=sp_, bb=bb, s2=s2, nq=nq: e.scalar_tensor_tensor(
                            out=sb_sb[s2][:, :nq], in0=sp_[:, :nq], scalar=float(scale), in1=b_sb[bb][:, :nq],
                            op0=ALU.mult, op1=ALU.add), [r_s[m][sbi], r_b[bb]], [r_sb[s2]])
                        P.act(lambda e, pp_=pp_, s2=s2, nq=nq: e.activation(
                            out=pp_[:, :nq], in_=sb_sb[s2][:, :nq], func=AF.Exp), [r_sb[s2]], [r_p[m][pbi]])
                    else:
                        P.act(lambda e, pp_=pp_, sp_=sp_, nq=nq: e.activation(
                            out=pp_[:, :nq], in_=sp_[:, :nq], func=AF.Exp, scale=float(scale)),
                            [r_s[m][sbi]], [r_p[m][pbi]])
                for m in range(maps):
                    sbi, pbi = slots[ki][m]
                    pp_ = p_sb[m][pbi]
                    P.pe(lambda e, m=m, hb=hb, kt=kt, pp_=pp_, nq=nq, ki=ki, nkt=nkt: e.matmul(
                        o_ps[m][:DV, :nq], lhsT=v_sb[hb][:, kt * DV:(kt + 1) * DV], rhs=pp_[:, :nq],
                        start=(ki == 0), stop=(ki == nkt - 1)), [r_kv[hb], r_p[m][pbi]], [r_o[m]])
                    P.pe(lambda e, m=m, pp_=pp_, nq=nq, ki=ki, nkt=nkt: e.matmul(
                        l_ps[m][:, :nq], lhsT=ones[:], rhs=pp_[:, :nq],
                        start=(ki == 0), stop=(ki == nkt - 1)), [r_ones, r_p[m][pbi]], [r_l[m]])

            emit_qk(0)
            for ki in range(nkt):
                if ki + 1 < nkt:
                    emit_qk(ki + 1)
                emit_rest(ki)
            ob = cnt["o"] % 2
            cnt["o"] += 1
            for m in range(maps):
                P.dve(lambda e, m=m, nq=nq: e.reciprocal(out=rec_sb[m][:DV, :nq], in_=l_ps[m][:DV, :nq]),
                      [r_l[m]], [r_rec[m]])
                dst = out_sb[ob] if (maps == 1) else t_sb[m]
                rdst = r_out[ob] if (maps == 1) else r_t[m]
                P.dve(lambda e, m=m, nq=nq, dst=dst: e.tensor_tensor(
                    out=dst[:DV, :nq], in0=o_ps[m][:DV, :nq], in1=rec_sb[m][:DV, :nq], op=ALU.mult),
                    [r_o[m], r_rec[m]], [rdst])
            if maps == 2:
                P.dve(lambda e, nq=nq, ob=ob: e.scalar_tensor_tensor(
                    out=out_sb[ob][:DV, :nq], in0=t_sb[1][:DV, :nq], scalar=neglam[:DV, 0:1], in1=t_sb[0][:DV, :nq],
                    op0=ALU.mult, op1=ALU.add), [r_t[0], r_t[1], r_lv], [r_out[ob]])
            out_toks.append(P.dma(oT[h, :, q0:q0 + nq], out_sb[ob][:DV, :nq], reads=[r_out[ob]]))
    return kb.finish(out_toks[-N_DMA_SEMS:])


EPS = 1e-6


class TB:
    def __init__(self, t, r=None):
        self.t = t
        self.r = r or Res()


class Ring:
    def __init__(self, kb, n, shape=None, dt=None, psum=False):
        self.items = [TB(kb.ps() if psum else kb.sb(shape, dt)) for _ in range(n)]
        self.i = 0

    def next(self):
        b = self.items[self.i % len(self.items)]
        self.i += 1
        return b


def host_consts():
    c = np.zeros((6, 128, 128), np.float32)
    c[0] = 1.0
    for g in range(2):
        c[1, g * 64:(g + 1) * 64, g * 64:(g + 1) * 64] = 1.0
    for g in range(4):
        c[2, g * 32:(g + 1) * 32, g * 32:(g + 1) * 32] = 1.0
    for idx, q in ((3, 64), (4, 16), (5, 8)):
        for m in range(128):
            src = m + q if (m % (2 * q)) < q else m - q
            c[idx, src, m] = 1.0
    return np.ascontiguousarray(c.transpose(1, 0, 2).reshape(128, 6 * 128)).astype(NBF)


def rope_tables(q, kind_cols, pos0, npos):
    pos = np.arange(pos0, pos0 + npos)
    row = (pos // 64).astype(np.float32)
    col = (pos % 64).astype(np.float32)
    inv = np.power(np.float32(10000.0), -np.arange(q, dtype=np.float32) / q).astype(np.float32)
    C = np.zeros((len(kind_cols), 128, npos), np.float32)
    S = np.zeros((len(kind_cols), 128, npos), np.float32)
    for t, kc in enumerate(kind_cols):
        for p in range(128):
            pp = row if kc[p] == 'r' else col
            ang = (pp * inv[p % q]).astype(np.float32)
            C[t, p] = np.cos(ang)
            sgn = -1.0 if (p % (2 * q)) < q else 1.0
            S[t, p] = sgn * np.sin(ang)
    return C, S


class TokCommon:
    def __init__(self, kb, T, consts_ap=None):
        self.kb = kb
        self.P = kb.P
        self.T = T
        P = self.P
        self.consts = consts_ap if consts_ap is not None else kb.din("consts", [128, 6 * 128], BF16)
        self.c_sb = TB(kb.sb([128, 6 * 128], BF16))
        P.dma(self.c_sb.t[:], self.consts[:, :], writes=[self.c_sb.r])
        self.stage = Ring(kb, 2, [128, 1024], F32)
        self.sq = Ring(kb, 3, [128, 512], BF16)
        self.ps_ss = Ring(kb, 2, psum=True)
        self.rstd = Ring(kb, 2, [128, 512], F32)
        self.tmp = Ring(kb, 3, [128, 512], F32)
        self.weng = 0
        self.epscol = {32: 0, 64: 1, 128: 2, 256: 3, 1024: 4, 512: 5}
        self.epsb = TB(kb.sb([128, 6], F32))
        for nf, col in self.epscol.items():
            P.pool(lambda e, nf=nf, col=col: e.memset(self.epsb.t[:, col:col + 1], float(nf * EPS)), [], [self.epsb.r])

    def cmat(self, i):
        return self.c_sb.t[:, i * 128:(i + 1) * 128]

    def load_w(self, w_ap, K, N, name=None):
        kb, P = self.kb, self.P
        KC = (K + 127) // 128
        wb = TB(kb.sb([128, KC, N], BF16))
        for k in range(KC):
            kr = min(128, K - k * 128)
            for c0 in range(0, N, 1024):
                cn = min(1024, N - c0)
                stg = self.stage.next()
                P.dma(stg.t[:kr, :cn], w_ap[k * 128:k * 128 + kr, c0:c0 + cn], writes=[stg.r])
                eng = "pool" if (self.weng % 2 == 0) else "dve"
                self.weng += 1
                P.op(eng, lambda e, stg=stg, k=k, c0=c0, cn=cn, kr=kr: e.tensor_copy(
                    out=wb.t[:kr, k, c0:c0 + cn], in_=stg.t[:kr, :cn]), [stg.r], [wb.r])
        return wb

    def modvecs(self, cv_ap, adaw_ap, adab_ap, nch, plus1):
        kb, P = self.kb, self.P
        cv = TB(kb.sb([128, 16], F32))
        ab = TB(kb.sb([128, nch], F32))
        mod = TB(kb.sb([128, nch, 2], F32))
        P.dma(cv.t[:], cv_ap[:, :], writes=[cv.r])
        P.dma(ab.t[:], adab_ap[:, :], writes=[ab.r])
        P.act(lambda e: e.activation(out=cv.t[:], in_=cv.t[:], func=AF.Silu), [cv.r], [cv.r])
        blk = Ring(kb, 2, [128, 8, 256], F32)
        psr = self.ps_ss
        for c0 in range(0, nch * 128, 256):
            b = blk.next()
            P.dma(b.t[:], adaw_ap[:, c0:c0 + 256].rearrange("(k p) c -> p k c", p=128), writes=[b.r])
            for mm in range(2):
                m = c0 // 128 + mm
                ps = psr.next()
                for k in range(8):
                    P.pe(lambda e, ps=ps, b=b, k=k, mm=mm: e.matmul(
                        ps.t[:, 0:2], lhsT=b.t[:, k, mm * 128:(mm + 1) * 128], rhs=cv.t[:, 2 * k:2 * k + 2],
                        start=(k == 0), stop=(k == 7)), [b.r, cv.r], [ps.r])
                add1 = 1.0 if m in plus1 else 0.0
                P.dve(lambda e, ps=ps, m=m, add1=add1: e.tensor_scalar(
                    out=mod.t[:, m, :], in0=ps.t[:, 0:2], scalar1=ab.t[:, m:m + 1], scalar2=add1,
                    op0=ALU.add, op1=ALU.add), [ps.r, ab.r], [mod.r])
        return mod

    def sumsq_rstd(self, srcs, n, blockmat, ntok, nfeat, src_res):
        P = self.P
        ps = self.ps_ss.next()
        for i, s in enumerate(srcs):
            sq = self.sq.next()
            P.act(lambda e, sq=sq, s=s: e.activation(out=sq.t[:, :ntok], in_=s, func=AF.Square), src_res, [sq.r])
            P.pe(lambda e, ps=ps, sq=sq, i=i: e.matmul(ps.t[:, :ntok], lhsT=blockmat, rhs=sq.t[:, :ntok],
                                                      start=(i == 0), stop=(i == len(srcs) - 1)),
                 [sq.r, self.c_sb.r], [ps.r])
        r = self.rstd.next()
        self.rsqrt(r, ps, 128, ntok, nfeat)
        return r

    def rsqrt(self, r, ps, np_, ntok, nfeat):
        P = self.P
        P.act(lambda e: e.activation(out=r.t[:np_, :ntok], in_=ps.t[:np_, :ntok], func=AF.Sqrt,
                                     bias=self.epsb.t[:np_, self.epscol[nfeat]:self.epscol[nfeat] + 1], scale=1.0),
              [ps.r, self.epsb.r], [r.r])
        P.dve(lambda e: e.reciprocal(out=r.t[:np_, :ntok], in_=r.t[:np_, :ntok]), [r.r], [r.r])

    def modulate(self, xs, mod, sh0, sc0, j, h, ntok):
        P = self.P
        r = self.sumsq_rstd([xs.t[:, k, :ntok] for k in range(8)], 8, self.cmat(0), ntok, 1024, [xs.r])
        for k in range(8):
            t = self.tmp.next()
            P.dve(lambda e, t=t, k=k, r=r: e.scalar_tensor_tensor(
                out=t.t[:, :ntok], in0=xs.t[:, k, :ntok], scalar=32.0, in1=r.t[:, :ntok],
                op0=ALU.mult, op1=ALU.mult), [xs.r, r.r], [t.r])
            P.act(lambda e, t=t, k=k: e.activation(
                out=h.t[:, k, :ntok], in_=t.t[:, :ntok], func=AF.Identity,
                scale=mod.t[:, sc0 + k, j:j + 1], bias=mod.t[:, sh0 + k, j:j + 1]), [t.r, mod.r], [h.r])


def build_pre(kind, T_lat, T_ctx):
    kb = KB()
    P = kb.P
    T = T_lat + T_ctx
    tc = TokCommon(kb, T)
    xT = kb.din("xT", [1024, T], F32)
    cv = kb.din("cv", [128, 16], F32)
    adaw = kb.din("adaw", [1024, 2048], F32)
    adab = kb.din("adab", [128, 16], F32)
    mod = tc.modvecs(cv, adaw, adab, 16, set(range(8, 16)))
    outs = {}
    ngain = {0: 0, 1: 2, 2: 2, 3: 7}[kind]
    gsc = {1: [8.0, 8.0], 2: [8.0, 8.0], 3: [16.0, 16.0, 128.0 ** 0.5, 32.0 ** 0.5, 8.0, 32.0 ** 0.5, 8.0]}.get(kind, [])
    if ngain:
        gains = kb.din("gains", [128, ngain], F32)
        g_sb = TB(kb.sb([128, ngain], F32))
        P.dma(g_sb.t[:], gains[:, :], writes=[g_sb.r])
        for i, s in enumerate(gsc):
            P.dve(lambda e, i=i, s=s: e.tensor_scalar(out=g_sb.t[:, i:i + 1], in0=g_sb.t[:, i:i + 1], scalar1=float(s),
                                                      scalar2=None, op0=ALU.mult), [g_sb.r], [g_sb.r])
    nrt = {0: 2, 1: 1, 2: 0, 3: 1}[kind]
    if nrt:
        ropeC = kb.din("ropeC", [nrt, 128, T_lat], F32)
        ropeS = kb.din("ropeS", [nrt, 128, T_lat], F32)
        rC = Ring(kb, 2, [128, nrt, 512], F32)
        rS = Ring(kb, 2, [128, nrt, 512], F32)
    if kind == 0:
        w = tc.load_w(kb.din("w_in", [1024, 6144], F32), 1024, 6144)
        for nm, F in (("qT", 1024), ("kT", 1024), ("vT", 2048), ("sgT", 2048)):
            outs[nm] = kb.dout(nm, [F, T], BF16)
    elif kind in (1, 2):
        w = tc.load_w(kb.din("w_in", [1024, 3072], F32), 1024, 3072)
        for nm in ("qT", "kT", "vT"):
            outs[nm] = kb.dout(nm, [1024, T], BF16)
    else:
        w = tc.load_w(kb.din("w_down", [1024, 416], F32), 1024, 416)
        w_uq = tc.load_w(kb.din("w_uq", [256, 1536], F32), 256, 1536)
        w_ukv = tc.load_w(kb.din("w_ukv", [128, 2048], F32), 128, 2048)
        for nm, F in (("qnT", 1024), ("qrT", 512), ("knT", 1024), ("krT", 32), ("vT", 1024)):
            outs[nm] = kb.dout(nm, [F, T], BF16)
    xs_r = Ring(kb, 2, [128, 8, 512], F32)
    h_r = Ring(kb, 2, [128, 8, 512], BF16)
    ps_r = Ring(kb, 3, psum=True)
    ps_x = Ring(kb, 2, psum=True)
    xb_r = Ring(kb, 2, [128, 512], BF16)
    ob_r = Ring(kb, 3, [128, 512], BF16)
    if kind == 3:
        ql_r = Ring(kb, 2, [128, 2, 512], BF16)
        ckv_r = Ring(kb, 2, [128, 512], BF16)
    toks = []
    tiles = [(c0, 512, 0) for c0 in range(0, T_lat, 512)] + ([(T_lat, T_ctx, 1)] if T_ctx else [])

    def proj(wt, kcs, col0, ncols, src, ntok):
        ps = ps_r.next()
        for i, k in enumerate(kcs):
            P.pe(lambda e, ps=ps, k=k, i=i: e.matmul(
                ps.t[:ncols, :ntok], lhsT=wt.t[:, k, col0:col0 + ncols], rhs=src.t[:, k, :ntok],
                start=(i == 0), stop=(i == len(kcs) - 1)), [wt.r, src.r], [ps.r])
        return ps

    def do_tile(c0, ntok, j):
        lat = (j == 0)
        xs = xs_r.next()
        P.dma(xs.t[:, :, :ntok], xT[:, c0:c0 + ntok].rearrange("(k p) t -> p k t", p=128), writes=[xs.r])
        if nrt and lat:
            cC, cS = rC.next(), rS.next()
            P.dma(cC.t[:], ropeC[:, :, c0:c0 + ntok].rearrange("n p t -> p n t"), writes=[cC.r])
            P.dma(cS.t[:], ropeS[:, :, c0:c0 + ntok].rearrange("n p t -> p n t"), writes=[cS.r])
        h = h_r.next()
        tc.modulate(xs, mod, 0, 8, j, h, ntok)

        def finish(ps, np_, dst, gn=None, pre_scale=None, rope=None, silu=False, keep=None):
            src = ps.t[:np_, :ntok]
            xn = None
            if gn is not None:
                bm, nfeat, gcol = gn
                r = tc.sumsq_rstd([src], 1, tc.cmat(bm)[:np_, :np_], ntok, nfeat, [ps.r]) if np_ == 128 else None
                if r is None:
                    r = tc.rstd.next()
                    sq = tc.sq.next()
                    pss = tc.ps_ss.next()
                    P.act(lambda e: e.activation(out=sq.t[:np_, :ntok], in_=src, func=AF.Square), [ps.r], [sq.r])
                    P.pe(lambda e: e.matmul(pss.t[:np_, :ntok], lhsT=tc.cmat(bm)[:np_, :np_], rhs=sq.t[:np_, :ntok],
                                            start=True, stop=True), [sq.r, tc.c_sb.r], [pss.r])
                    tc.rsqrt(r, pss, np_, ntok, nfeat)
                xn = tc.tmp.next()
                P.dve(lambda e: e.scalar_tensor_tensor(
                    out=xn.t[:np_, :ntok], in0=src, scalar=g_sb.t[:np_, gcol:gcol + 1], in1=r.t[:np_, :ntok],
                    op0=ALU.mult, op1=ALU.mult), [ps.r, r.r, g_sb.r], [xn.r])
            elif (rope is not None and lat) or pre_scale is not None:
                xn = tc.tmp.next()
                P.act(lambda e: e.activation(out=xn.t[:np_, :ntok], in_=src, func=AF.Identity,
                                             scale=float(pre_scale or 1.0)), [ps.r], [xn.r])
            if keep is not None:
                ob, oap = keep
            else:
                ob = ob_r.next()
                oap = ob.t[:np_, :ntok]
            if rope is not None and lat:
                pm, ty = rope
                xb = xb_r.next()
                P.act(lambda e: e.activation(out=xb.t[:np_, :ntok], in_=xn.t[:np_, :ntok], func=AF.Copy), [xn.r], [xb.r])
                px = ps_x.next()
                P.pe(lambda e: e.matmul(px.t[:np_, :ntok], lhsT=tc.cmat(pm)[:np_, :np_], rhs=xb.t[:np_, :ntok],
                                        start=True, stop=True), [xb.r, tc.c_sb.r], [px.r])
                t1 = tc.tmp.next()
                t2 = tc.tmp.next()
                P.pool(lambda e: e.tensor_tensor(out=t1.t[:np_, :ntok], in0=xn.t[:np_, :ntok],
                                                 in1=cC.t[:np_, ty, :ntok], op=ALU.mult), [xn.r, cC.r], [t1.r])
                P.dve(lambda e: e.tensor_tensor(out=t2.t[:np_, :ntok], in0=px.t[:np_, :ntok],
                                                in1=cS.t[:np_, ty, :ntok], op=ALU.mult), [px.r, cS.r], [t2.r])
                P.pool(lambda e: e.tensor_tensor(out=oap, in0=t1.t[:np_, :ntok], in1=t2.t[:np_, :ntok], op=ALU.add),
                       [t1.r, t2.r], [ob.r])
            elif xn is not None:
                P.pool(lambda e: e.tensor_copy(out=oap, in_=xn.t[:np_, :ntok]), [xn.r], [ob.r])
            else:
                P.act(lambda e: e.activation(out=oap, in_=src, func=(AF.Silu if silu else AF.Copy)), [ps.r], [ob.r])
            if keep is None:
                toks.append(P.dma(dst, oap, reads=[ob.r]))

        K8 = list(range(8))
        if kind == 0:
            for m in range(48):
                ps = proj(w, K8, m * 128, 128, h, ntok)
                if m < 8:
                    finish(ps, 128, outs["qT"][m * 128:(m + 1) * 128, c0:c0 + ntok], rope=(3, m % 2))
                elif m < 16:
                    finish(ps, 128, outs["kT"][(m - 8) * 128:(m - 7) * 128, c0:c0 + ntok], rope=(3, m % 2),
                           pre_scale=1.0 / 16.0)
                elif m < 32:
                    finish(ps, 128, outs["vT"][(m - 16) * 128:(m - 15) * 128, c0:c0 + ntok])
                else:
                    finish(ps, 128, outs["sgT"][(m - 32) * 128:(m - 31) * 128, c0:c0 + ntok], silu=True)
        elif kind in (1, 2):
            for m in range(24):
                ps = proj(w, K8, m * 128, 128, h, ntok)
                nm = ("qT", "kT", "vT")[m // 8]
                dst = outs[nm][(m % 8) * 128:(m % 8 + 1) * 128, c0:c0 + ntok]
                if m < 16:
                    finish(ps, 128, dst, gn=(1, 64, m // 8), rope=((4, 0) if kind == 1 else None))
                else:
                    finish(ps, 128, dst)
        else:
            p0 = proj(w, K8, 0, 128, h, ntok)
            p1 = proj(w, K8, 128, 128, h, ntok)
            r = tc.sumsq_rstd([p0.t[:, :ntok], p1.t[:, :ntok]], 2, tc.cmat(0), ntok, 256, [p0.r, p1.r])
            ql = ql_r.next()
            for k, pp in enumerate((p0, p1)):
                P.dve(lambda e, k=k, pp=pp: e.scalar_tensor_tensor(
                    out=ql.t[:, k, :ntok], in0=pp.t[:, :ntok], scalar=g_sb.t[:, k:k + 1], in1=r.t[:, :ntok],
                    op0=ALU.mult, op1=ALU.mult), [pp.r, r.r, g_sb.r], [ql.r])
            p2 = proj(w, K8, 256, 128, h, ntok)
            ckv = ckv_r.next()
            finish(p2, 128, None, gn=(0, 128, 2), keep=(ckv, ckv.t[:, :ntok]))
            p3 = proj(w, K8, 384, 32, h, ntok)
            finish(p3, 32, outs["krT"][0:32, c0:c0 + ntok], gn=(2, 32, 3), rope=(5, 0))
            ckv3 = TB(ckv.t.rearrange("p (o t) -> p o t", o=1) if False else ckv.t, ckv.r)
            for m in range(12):
                ps = proj(w_uq, [0, 1], m * 128, 128, ql, ntok)
                if m < 8:
                    finish(ps, 128, outs["qnT"][m * 128:(m + 1) * 128, c0:c0 + ntok], gn=(1, 64, 4))
                else:
                    finish(ps, 128, outs["qrT"][(m - 8) * 128:(m - 7) * 128, c0:c0 + ntok], gn=(2, 32, 5), rope=(5, 0))
            for m in range(16):
                ps = ps_r.next()
                P.pe(lambda e, ps=ps, m=m: e.matmul(ps.t[:, :ntok], lhsT=w_ukv.t[:, 0, m * 128:(m + 1) * 128],
                                                   rhs=ckv.t[:, :ntok], start=True, stop=True), [w_ukv.r, ckv.r], [ps.r])
                if m < 8:
                    finish(ps, 128, outs["knT"][m * 128:(m + 1) * 128, c0:c0 + ntok], gn=(1, 64, 6))
                else:
                    finish(ps, 128, outs["vT"][(m - 8) * 128:(m - 7) * 128, c0:c0 + ntok])
    for (c0_, ntok_, j_) in tiles:
        do_tile(c0_, ntok_, j_)
    return kb.finish(toks[-N_DMA_SEMS:])


def mla_host_weights(wd, wuq, wukv, qg, kvg, gq, gk):
    cn = [h * 96 + i for h in range(16) for i in range(64)]
    cr = [h * 96 + 64 + i for h in range(16) for i in range(32)]
    kn = [h * 128 + i for h in range(16) for i in range(64)]
    vv = [h * 128 + 64 + i for h in range(16) for i in range(64)]
    gains = np.stack([qg[:128], qg[128:], kvg, np.tile(gk[64:], 4), np.tile(gq[:64], 2), np.tile(gq[64:], 4),
                      np.tile(gk[:64], 2)], 1).astype(np.float32)
    return {"w_down": np.ascontiguousarray(wd), "w_uq": np.ascontiguousarray(wuq[:, cn + cr]),
            "w_ukv": np.ascontiguousarray(wukv[:, kn + vv]), "gains": np.ascontiguousarray(gains)}


def build_post(kind, T_lat, T_ctx, lam_init=0.0):
    kb = KB()
    T = T_lat + T_ctx
    Fa = 2048 if kind == 0 else 1024
    KA = Fa // 128
    xT = kb.din("xT", [1024, T], F32)
    a_dt = BF16 if kind == 0 else F32
    aT = kb.din("aT", [Fa, T], a_dt)
    w_out = kb.din("w_out", [Fa, 1024], F32)
    cv = kb.din("cv", [128, 16], F32)
    adaw = kb.din("adaw", [1024, 4096], F32)
    adab = kb.din("adab", [128, 32], F32)
    w1 = kb.din("ffn_w_in", [1024, 5632], F32)
    w2 = kb.din("ffn_w_out", [2816, 1024], F32)
    x1T = kb.dout("x1T", [1024, T], F32)
    xoT = kb.dout("xoT", [1024, T], F32)
    if kind == 0:
        sgT = kb.din("sgT", [2048, T], BF16)
        ng = kb.din("ng", [128, 16], F32)
    if kind == 1:
        sg = kb.din("sg", [128, 1], F32)
    P = kb.P
    tc = TokCommon(kb, T)
    consts_ap = tc.consts
    mod = tc.modvecs(cv, adaw[:, 0:1024], adab[:, 0:8], 8, set())
    wo = tc.load_w(w_out, Fa, 1024)
    if kind == 0:
        ng_sb = TB(kb.sb([128, 16], F32))
        P.dma(ng_sb.t[:], ng[:, :], writes=[ng_sb.r])
    if kind == 1:
        sg_sb = TB(kb.sb([128, 1], F32))
        P.dma(sg_sb.t[:], sg[:, :], writes=[sg_sb.r])
        P.dve(lambda e: e.tensor_scalar(out=sg_sb.t[:], in0=sg_sb.t[:], scalar1=float((1.0 - lam_init) * 128.0 ** 0.5),
                                        scalar2=None, op0=ALU.mult), [sg_sb.r], [sg_sb.r])
    xs_r = Ring(kb, 2, [128, 8, 512], F32)
    a_r = Ring(kb, 2, [128, KA, 512], a_dt)
    ab_r = Ring(kb, 2, [128, KA, 512], BF16)
    if kind == 0:
        sg_r = Ring(kb, 2, [128, KA, 512], BF16)
    ps_r = Ring(kb, 3, psum=True)
    toks = []
    tiles = [(c0, 512, 0) for c0 in range(0, T_lat, 512)] + ([(T_lat, T_ctx, 1)] if T_ctx else [])

    def tile_a(c0, ntok, j):
        xs = xs_r.next()
        P.dma(xs.t[:, :, :ntok], xT[:, c0:c0 + ntok].rearrange("(k p) t -> p k t", p=128), writes=[xs.r])
        a = a_r.next()
        P.dma(a.t[:, :, :ntok], aT[:, c0:c0 + ntok].rearrange("(k p) t -> p k t", p=128), writes=[a.r])
        ab = ab_r.next()
        if kind == 0:
            sgt = sg_r.next()
            P.dma(sgt.t[:, :, :ntok], sgT[:, c0:c0 + ntok].rearrange("(k p) t -> p k t", p=128), writes=[sgt.r])
            for k in range(KA):
                P.dve(lambda e, k=k: e.scalar_tensor_tensor(
                    out=ab.t[:, k, :ntok], in0=a.t[:, k, :ntok], scalar=ng_sb.t[:, k:k + 1], in1=sgt.t[:, k, :ntok],
                    op0=ALU.mult, op1=ALU.mult), [a.r, sgt.r, ng_sb.r], [ab.r])
        elif kind == 1:
            for k in range(KA):
                r = tc.sumsq_rstd([a.t[:, k, :ntok]], 1, tc.cmat(0), ntok, 128, [a.r])
                P.dve(lambda e, k=k, r=r: e.scalar_tensor_tensor(
                    out=ab.t[:, k, :ntok], in0=a.t[:, k, :ntok], scalar=sg_sb.t[:, 0:1], in1=r.t[:, :ntok],
                    op0=ALU.mult, op1=ALU.mult), [a.r, r.r, sg_sb.r], [ab.r])
        else:
            for k in range(KA):
                P.pool(lambda e, k=k: e.tensor_copy(out=ab.t[:, k, :ntok], in_=a.t[:, k, :ntok]), [a.r], [ab.r])
        for m in range(8):
            ps = ps_r.next()
            for k in range(KA):
                P.pe(lambda e, ps=ps, k=k, m=m: e.matmul(
                    ps.t[:, :ntok], lhsT=wo.t[:, k, m * 128:(m + 1) * 128], rhs=ab.t[:, k, :ntok],
                    start=(k == 0), stop=(k == KA - 1)), [wo.r, ab.r], [ps.r])
            P.dve(lambda e, ps=ps, m=m: e.scalar_tensor_tensor(
                out=xs.t[:, m, :ntok], in0=ps.t[:, :ntok], scalar=mod.t[:, m, j:j + 1], in1=xs.t[:, m, :ntok],
                op0=ALU.mult, op1=ALU.add), [ps.r, mod.r, xs.r], [xs.r])
        toks.append(P.dma(x1T[:, c0:c0 + ntok].rearrange("(k p) t -> p k t", p=128), xs.t[:, :, :ntok], reads=[xs.r]))

    for t in tiles:
        tile_a(*t)
    kb.next_phase(toks)
    P = kb.P
    tc = TokCommon(kb, T, consts_ap)
    mod = tc.modvecs(cv, adaw[:, 1024:4096], adab[:, 8:32], 24, set(range(8, 16)))
    w1s = tc.load_w(w1, 1024, 5632)
    w2s = tc.load_w(w2, 2816, 1024)
    NT = 256
    xs_r = Ring(kb, 2, [128, 8, NT], F32)
    h_r = Ring(kb, 1, [128, 8, NT], BF16)
    act_r = Ring(kb, 1, [128, 22, NT], BF16)
    ps_a = Ring(kb, 2, psum=True)
    ps_g = Ring(kb, 2, psum=True)
    ps_o = Ring(kb, 2, psum=True)
    toks = []
    tiles = [(c0, NT, 0) for c0 in range(0, T_lat, NT)] + ([(T_lat, T_ctx, 1)] if T_ctx else [])

    def tile_b(c0, ntok, j):
        xs = xs_r.next()
        P.dma(xs.t[:, :, :ntok], x1T[:, c0:c0 + ntok].rearrange("(k p) t -> p k t", p=128), writes=[xs.r])
        h = h_r.next()
        tc.modulate(xs, mod, 0, 8, j, h, ntok)
        act = act_r.next()
        for jj in range(22):
            pa, pg = ps_a.next(), ps_g.next()
            for k in range(8):
                P.pe(lambda e, pa=pa, k=k, jj=jj: e.matmul(
                    pa.t[:, :ntok], lhsT=w1s.t[:, k, jj * 128:(jj + 1) * 128], rhs=h.t[:, k, :ntok],
                    start=(k == 0), stop=(k == 7)), [w1s.r, h.r], [pa.r])
            for k in range(8):
                P.pe(lambda e, pg=pg, k=k, jj=jj: e.matmul(
                    pg.t[:, :ntok], lhsT=w1s.t[:, k, 2816 + jj * 128:2816 + (jj + 1) * 128], rhs=h.t[:, k, :ntok],
                    start=(k == 0), stop=(k == 7)), [w1s.r, h.r], [pg.r])
            sa = tc.tmp.next()
            P.act(lambda e, sa=sa, pa=pa: e.activation(out=sa.t[:, :ntok], in_=pa.t[:, :ntok], func=AF.Silu),
                  [pa.r], [sa.r])
            P.dve(lambda e, sa=sa, pg=pg, jj=jj: e.tensor_tensor(
                out=act.t[:, jj, :ntok], in0=sa.t[:, :ntok], in1=pg.t[:, :ntok], op=ALU.mult), [sa.r, pg.r], [act.r])
        for m in range(8):
            ps = ps_o.next()
            for jj in range(22):
                P.pe(lambda e, ps=ps, jj=jj, m=m: e.matmul(
                    ps.t[:, :ntok], lhsT=w2s.t[:, jj, m * 128:(m + 1) * 128], rhs=act.t[:, jj, :ntok],
                    start=(jj == 0), stop=(jj == 21)), [w2s.r, act.r], [ps.r])
            P.dve(lambda e, ps=ps, m=m: e.scalar_tensor_tensor(
                out=xs.t[:, m, :ntok], in0=ps.t[:, :ntok], scalar=mod.t[:, 16 + m, j:j + 1], in1=xs.t[:, m, :ntok],
                op0=ALU.mult, op1=ALU.add), [ps.r, mod.r, xs.r], [xs.r])
        toks.append(P.dma(xoT[:, c0:c0 + ntok].rearrange("(k p) t -> p k t", p=128), xs.t[:, :, :ntok], reads=[xs.r]))

    for t in tiles:
        tile_b(*t)
    return kb.finish(toks[-N_DMA_SEMS:])


def ret_tables():
    j = np.arange(128, dtype=np.float32)[:, None]
    i = np.arange(128, dtype=np.float32)[None, :]
    t = np.zeros((8, 128, 128), np.float32)
    t[0] = np.maximum(i - j, 0)
    t[1] = (i >= j)
    t[2] = np.maximum(j - i, 0)
    t[3] = (j >= i)
    t[4] = np.broadcast_to(i + 1, (128, 128))
    t[5] = np.broadcast_to(128 - i, (128, 128))
    t[6] = 128 + i - j
    t[7] = 128 + j - i
    p = np.arange(128, dtype=np.float32)
    cols = np.stack([127 - p, p, 255 - p, 128 + p], 1)
    return (np.ascontiguousarray(t.transpose(1, 0, 2).reshape(128, 8 * 128)), np.ascontiguousarray(cols))


def build_ret(NG=32, with_ctx=True):
    kb = KB()
    P = kb.P
    NCH = NG * 4
    qT_in = kb.din("qT", [NG, 128, 1024], BF16)
    kT_in = kb.din("kT", [NG, 128, 1024], BF16)
    kt_in = kb.din("ktok", [NG, 128, 1024], BF16)
    vt_in = kb.din("vtok", [NG, 128, 2048], BF16)
    qc_in = kb.din("qcT", [128, 512], BF16)
    kc_in = kb.din("kcT", [128, 512], BF16)
    kct_in = kb.din("kctok", [128, 512], BF16)
    vct_in = kb.din("vctok", [128, 1024], BF16)
    dl_in = kb.din("dl", [128, 2], F32)
    tab_in = kb.din("tabs", [128, 1024], F32)
    col_in = kb.din("cols", [128, 4], F32)
    on_out = kb.dout("on", [NCH + 2, 128, 512], BF16)
    obD = kb.nc.dram_tensor("obD", [NCH, 128, 512], F32, kind="Internal").ap()
    r_obD = [Res() for _ in range(NCH)]

    tabs = TB(kb.sb([128, 1024], F32))
    cols = TB(kb.sb([128, 4], F32))
    lg = TB(kb.sb([128, 2], F32))
    one = TB(kb.sb([128, 1], F32))
    epsb = TB(kb.sb([128, 1], F32))
    P.dma(tabs.t[:], tab_in[:, :], writes=[tabs.r])
    P.dma(cols.t[:], col_in[:, :], writes=[cols.r])
    P.dma(lg.t[:], dl_in[:, :], writes=[lg.r])
    P.pool(lambda e: e.memset(one.t[:], 1.0), [], [one.r])
    P.pool(lambda e: e.memset(epsb.t[:], EPS), [], [epsb.r])
    P.act(lambda e: e.activation(out=lg.t[:], in_=lg.t[:], func=AF.Exp, scale=-1.0), [lg.r], [lg.r])
    P.act(lambda e: e.activation(out=lg.t[:], in_=lg.t[:], func=AF.Ln, bias=one.t[:, 0:1], scale=1.0), [lg.r, one.r], [lg.r])
    P.dve(lambda e: e.tensor_scalar(out=lg.t[:], in0=lg.t[:], scalar1=-1.0, scalar2=None, op0=ALU.mult), [lg.r], [lg.r])
    cst = TB(kb.sb([128, 8 * 128], F32))
    pc = TB(kb.sb([128, 8], F32))

    def tab(i):
        return tabs.t[:, i * 128:(i + 1) * 128]

    def ct(i):
        return cst.t[:, i * 128:(i + 1) * 128]

    def expt(dst, src, d):
        P.act(lambda e: e.activation(out=dst, in_=src, func=AF.Exp, scale=lg.t[:, d:d + 1]), [tabs.r, lg.r, cols.r], [cst.r])

    expt(ct(5), tab(0), 0)
    expt(ct(6), tab(2), 1)
    P.dve(lambda e: e.tensor_tensor(out=ct(5), in0=ct(5), in1=tab(1), op=ALU.mult), [cst.r, tabs.r], [cst.r])
    P.dve(lambda e: e.tensor_tensor(out=ct(6), in0=ct(6), in1=tab(3), op=ALU.mult), [cst.r, tabs.r], [cst.r])
    P.dve(lambda e: e.tensor_tensor(out=ct(0), in0=ct(5), in1=ct(6), op=ALU.add), [cst.r], [cst.r])
    expt(ct(1), tab(4), 0)
    expt(ct(2), tab(5), 1)
    expt(ct(3), tab(6), 0)
    expt(ct(4), tab(7), 1)
    for dcol, scol, d in ((0, 0, 0), (1, 1, 1), (2, 2, 0), (3, 3, 1)):
        P.act(lambda e, dcol=dcol, scol=scol, d=d: e.activation(
            out=pc.t[:, dcol:dcol + 1], in_=cols.t[:, scol:scol + 1], func=AF.Exp, scale=lg.t[:, d:d + 1]),
            [cols.r, lg.r], [pc.r])
    for d in (0, 1):
        P.act(lambda e, d=d: e.activation(out=pc.t[:, 4 + d:5 + d], in_=lg.t[:, d:d + 1], func=AF.Exp, scale=128.0),
              [lg.r], [pc.r])
    DS = {0: 0, 1: 1}
    DCH = {0: 4, 1: 5}
    DC = {0: 1, 1: 2}

    S = [TB(kb.sb([128, 2, 512], F32)) for _ in range(2)]
    Sb = [TB(kb.sb([128, 2, 512], BF16)) for _ in range(2)]
    ps_att = Ring(kb, 2, psum=True)
    ps_o = Ring(kb, 2, psum=True)
    ps_u = Ring(kb, 4, psum=True)
    a_r = Ring(kb, 2, [128, 128], BF16)
    qs_r = Ring(kb, 2, [128, 2, 128], BF16)
    ks_r = Ring(kb, 2, [128, 256], BF16)
    o_r = Ring(kb, 2, [128, 512], F32)
    xc_r = Ring(kb, 2, [128, 512], F32)
    sq_r = Ring(kb, 2, [128, 512], F32)
    st_r = Ring(kb, 2, [128, 4], F32)
    on_r = Ring(kb, 3, [128, 512], BF16)
    ob_r = Ring(kb, 3, [128, 512], F32)
    qg_r = Ring(kb, 2, [128, 1024], BF16)
    kg_r = Ring(kb, 2, [128, 1024], BF16)
    ktg_r = Ring(kb, 2, [128, 1024], BF16)
    vg_r = Ring(kb, 2, [128, 2048], BF16)
    toks = []

    def norm_out(o, chunk):
        stt = st_r.next()
        P.dve(lambda e: e.reduce_sum(out=stt.t[:, 0:1], in_=o.t[:], axis=AX.X), [o.r], [stt.r])
        P.dve(lambda e: e.tensor_scalar(out=stt.t[:, 0:1], in0=stt.t[:, 0:1], scalar1=-1.0 / 512, scalar2=None,
                                        op0=ALU.mult), [stt.r], [stt.r])
        xc = xc_r.next()
        P.dve(lambda e: e.tensor_scalar(out=xc.t[:], in0=o.t[:], scalar1=stt.t[:, 0:1], scalar2=None, op0=ALU.add),
              [o.r, stt.r], [xc.r])
        sq = sq_r.next()
        P.act(lambda e: e.activation(out=sq.t[:], in_=xc.t[:], func=AF.Square), [xc.r], [sq.r])
        P.dve(lambda e: e.reduce_sum(out=stt.t[:, 1:2], in_=sq.t[:], axis=AX.X), [sq.r, stt.r], [stt.r])
        P.act(lambda e: e.activation(out=stt.t[:, 2:3], in_=stt.t[:, 1:2], func=AF.Sqrt, bias=epsb.t[:, 0:1],
                                     scale=1.0 / 512), [stt.r, epsb.r], [stt.r])
        P.dve(lambda e: e.reciprocal(out=stt.t[:, 3:4], in_=stt.t[:, 2:3]), [stt.r], [stt.r])
        ob = on_r.next()
        P.act(lambda e: e.activation(out=ob.t[:], in_=xc.t[:], func=AF.Copy, scale=stt.t[:, 3:4]),
              [xc.r, stt.r], [ob.r])
        toks.append(P.dma(on_out[chunk], ob.t[:], reads=[ob.r]))

    def state_update(d, ktile_ap, ktile_res, v_ap, v_res, scale_ap, first=False):
        ks = ks_r.next()
        P.dve(lambda e: e.tensor_scalar(out=ks.t[:], in0=ktile_ap, scalar1=scale_ap, scalar2=None, op0=ALU.mult),
              [ktile_res, pc.r], [ks.r])
        return ks

    def upd(d, ks_list, v_list, v_res, init):
        for dch in range(2):
            pu = ps_u.next()
            n = len(ks_list)
            for i in range(n):
                P.pe(lambda e, pu=pu, i=i, dch=dch: e.matmul(
                    pu.t[:, :], lhsT=ks_list[i].t[:, dch * 128:(dch + 1) * 128], rhs=v_list[i],
                    start=(i == 0), stop=(i == n - 1)), [ks_list[i].r, v_res], [pu.r])
            if init:
                P.dve(lambda e, pu=pu, dch=dch: e.tensor_copy(out=S[d].t[:, dch, :], in_=pu.t[:, :]), [pu.r], [S[d].r])
            else:
                P.dve(lambda e, pu=pu, dch=dch: e.scalar_tensor_tensor(
                    out=S[d].t[:, dch, :], in0=S[d].t[:, dch, :], scalar=pc.t[:, DCH[d]:DCH[d] + 1], in1=pu.t[:, :],
                    op0=ALU.mult, op1=ALU.add), [pu.r, S[d].r, pc.r], [S[d].r])
            P.pool(lambda e, dch=dch: e.tensor_copy(out=Sb[d].t[:, dch, :], in_=S[d].t[:, dch, :]), [S[d].r], [Sb[d].r])

    qc = TB(kb.sb([128, 512], BF16))
    kc = TB(kb.sb([128, 512], BF16))
    kct = TB(kb.sb([128, 512], BF16))
    vct = TB(kb.sb([128, 1024], BF16))
    for t_, src in ((qc, qc_in), (kc, kc_in), (kct, kct_in), (vct, vct_in)):
        P.dma(t_.t[:], src[:, :], writes=[t_.r])
    wcol = {0: (2, 0), 1: (1, 3)}
    for d in (0, 1):
        ksl = []
        for tt in range(2):
            ksl.append(state_update(d, kct.t[:, tt * 256:(tt + 1) * 256], kct.r, None, None,
                                    pc.t[:, wcol[d][tt]:wcol[d][tt] + 1]))
        upd(d, ksl, [vct.t[:, 0:512], vct.t[:, 512:1024]], vct.r, True)
    if with_ctx:
        for it in range(2):
            al = []
            for jt in range(2):
                pa = ps_att.next()
                for dch in range(2):
                    P.pe(lambda e, pa=pa, dch=dch, jt=jt, it=it: e.matmul(
                        pa.t[:, :128], lhsT=kc.t[:, dch * 256 + jt * 128:dch * 256 + (jt + 1) * 128],
                        rhs=qc.t[:, dch * 256 + it * 128:dch * 256 + (it + 1) * 128],
                        start=(dch == 0), stop=(dch == 1)), [kc.r, qc.r], [pa.r])
                a = a_r.next()
                dtab = ct(0) if it == jt else (ct(3) if it > jt else ct(4))
                P.dve(lambda e, a=a, pa=pa, dtab=dtab: e.tensor_tensor(out=a.t[:], in0=pa.t[:, :128], in1=dtab, op=ALU.mult),
                      [pa.r, cst.r], [a.r])
                al.append(a)
            po = ps_o.next()
            for jt in range(2):
                P.pe(lambda e, po=po, jt=jt: e.matmul(po.t[:, :], lhsT=al[jt].t[:], rhs=vct.t[:, jt * 512:(jt + 1) * 512],
                                                      start=(jt == 0), stop=(jt == 1)), [al[jt].r, vct.r], [po.r])
            o = o_r.next()
            P.act(lambda e, o=o, po=po: e.activation(out=o.t[:], in_=po.t[:, :], func=AF.Copy), [po.r], [o.r])
            norm_out(o, NCH + it)

    def load_group(g, need_kT):
        qg, ktg, vg = qg_r.next(), ktg_r.next(), vg_r.next()
        P.dma(qg.t[:], qT_in[g], writes=[qg.r])
        P.dma(ktg.t[:], kt_in[g], writes=[ktg.r])
        P.dma(vg.t[:], vt_in[g], writes=[vg.r])
        kg = None
        if need_kT:
            kg = kg_r.next()
            P.dma(kg.t[:], kT_in[g], writes=[kg.r])
        return qg, kg, ktg, vg

    def scaled_q(qg, cc, d):
        qs = qs_r.next()
        for dch in range(2):
            P.pool(lambda e, dch=dch: e.tensor_tensor(
                out=qs.t[:, dch, :], in0=qg.t[:, dch * 512 + cc * 128:dch * 512 + (cc + 1) * 128], in1=ct(DC[d]),
                op=ALU.mult), [qg.r, cst.r], [qs.r])
        return qs

    def bwd_chunk(g, cc, grp):
        qg, kg, ktg, vg = grp
        c = 4 * g + cc
        qs = scaled_q(qg, cc, 1)
        po = ps_o.next()
        for dch in range(2):
            P.pe(lambda e, dch=dch: e.matmul(po.t[:, :], lhsT=qs.t[:, dch, :], rhs=Sb[1].t[:, dch, :],
                                             start=(dch == 0), stop=(dch == 1)), [qs.r, Sb[1].r], [po.r])
        ob = ob_r.next()
        P.act(lambda e: e.activation(out=ob.t[:], in_=po.t[:, :], func=AF.Copy), [po.r], [ob.r])
        P.dma(obD[c], ob.t[:], reads=[ob.r], writes=[r_obD[c]])
        ks = state_update(1, ktg.t[:, cc * 256:(cc + 1) * 256], ktg.r, None, None, pc.t[:, DS[1]:DS[1] + 1])
        upd(1, [ks], [vg.t[:, cc * 512:(cc + 1) * 512]], vg.r, False)

    for g in range(NG - 1, -1, -1):
        grp = load_group(g, False)
        for cc in range(3, -1, -1):
            bwd_chunk(g, cc, grp)

    def fwd_chunk(g, cc, grp):
        qg, kg, ktg, vg = grp
        c = 4 * g + cc
        ob = ob_r.next()
        P.dma(ob.t[:], obD[c], reads=[r_obD[c]], writes=[ob.r])
        pa = ps_att.next()
        for dch in range(2):
            P.pe(lambda e, dch=dch: e.matmul(
                pa.t[:, :128], lhsT=kg.t[:, dch * 512 + cc * 128:dch * 512 + (cc + 1) * 128],
                rhs=qg.t[:, dch * 512 + cc * 128:dch * 512 + (cc + 1) * 128],
                start=(dch == 0), stop=(dch == 1)), [kg.r, qg.r], [pa.r])
        a = a_r.next()
        P.dve(lambda e: e.tensor_tensor(out=a.t[:], in0=pa.t[:, :128], in1=ct(0), op=ALU.mult), [pa.r, cst.r], [a.r])
        qs = scaled_q(qg, cc, 0)
        po = ps_o.next()
        P.pe(lambda e: e.matmul(po.t[:, :], lhsT=a.t[:], rhs=vg.t[:, cc * 512:(cc + 1) * 512], start=True, stop=False),
             [a.r, vg.r], [po.r])
        for dch in range(2):
            P.pe(lambda e, dch=dch: e.matmul(po.t[:, :], lhsT=qs.t[:, dch, :], rhs=Sb[0].t[:, dch, :],
                                             start=False, stop=(dch == 1)), [qs.r, Sb[0].r], [po.r])
        o = o_r.next()
        P.dve(lambda e: e.tensor_tensor(out=o.t[:], in0=po.t[:, :], in1=ob.t[:], op=ALU.add), [po.r, ob.r], [o.r])
        norm_out(o, c)
        ks = state_update(0, ktg.t[:, cc * 256:(cc + 1) * 256], ktg.r, None, None, pc.t[:, DS[0]:DS[0] + 1])
        upd(0, [ks], [vg.t[:, cc * 512:(cc + 1) * 512]], vg.r, False)

    for g in range(NG):
        grp = load_group(g, True)
        for cc in range(4):
            fwd_chunk(g, cc, grp)
    return kb.finish(toks[-N_DMA_SEMS:])


NCORES = 8
TL, TCX = 4096, 64
SEQ, NCTX, DM = 16384, 256, 1024
_LAUNCHES = []


def _run(nc, in_maps):
    res = run_bass_kernel_spmd(nc, in_maps, core_ids=list(range(NCORES)))
    return res.results


def _cvpack(c, cc):
    a_ = np.stack([c, cc], 0).reshape(2, 8, 128)
    return np.ascontiguousarray(a_.transpose(2, 1, 0).reshape(128, 16)).astype(np.float32)


def _adab_pack(b, m0, nch):
    return np.ascontiguousarray(b[m0 * 128:(m0 + nch) * 128].reshape(nch, 128).T).astype(np.float32)


def _gather_batch(outs, name, b):
    lat = np.concatenate([outs[b * 4 + r][name][:, :TL] for r in range(4)], 1)
    ctx = np.concatenate([outs[b * 4 + r][name][:, TL:] for r in range(4)], 1)
    return lat, ctx


def _vpack(v_tok, dv):
    nk = v_tok.shape[0]
    return np.ascontiguousarray(v_tok.reshape(nk // 128, 128, dv).transpose(1, 0, 2).reshape(128, -1))


def na_bias_table(rel_bias):
    out = np.full((18, 16, 128, 256), -30000.0, np.float32)
    kk = np.arange(128)
    qq = np.arange(256)
    for var, j in ((0, 0), (1, 1), (2, 63)):
        for t in range(6):
            kr = (4 * j - 4 + 2 * t + kk // 64)[:, None]
            kc = (kk % 64)[:, None]
            qrow = (4 * j + qq // 64)[None, :]
            qc = (qq % 64)[None, :]
            r0 = np.clip(qrow - 4, 0, 248)
            c0 = np.clip(qc - 8, 0, 48)
            valid = (kr >= 0) & (kr <= 255) & (kr >= r0) & (kr < r0 + 8) & (kc >= c0) & (kc < c0 + 16)
            dr = np.clip(kr - qrow + 7, 0, 14)
            dc = np.clip(kc - qc + 15, 0, 30)
            vals = rel_bias[:, dr, dc]
            out[var * 6 + t] = np.where(valid[None], vals, np.float32(-30000.0))
    return out


def _pre_common(i, xs, inp):
    ims = []
    for c in range(NCORES):
        b = c // 4
        ims.append({"xT": xs[c], "cv": _cvpack(inp["c"][b], inp["c_ctx"]),
                    "adaw": np.ascontiguousarray(inp["ada_w"][i][:, :2048]),
                    "adab": _adab_pack(inp["ada_b"][i], 0, 16), "consts": host_consts()})
    return ims


def _post_common(i, xs, inp, T):
    ims = []
    for c in range(NCORES):
        b = c // 4
        ims.append({"xT": np.ascontiguousarray(xs[c][:, :T]), "cv": _cvpack(inp["c"][b], inp["c_ctx"]),
                    "adaw": np.ascontiguousarray(inp["ada_w"][i][:, 2048:]),
                    "adab": _adab_pack(inp["ada_b"][i], 16, 32), "consts": host_consts(),
                    "ffn_w_in": inp["ffn_w_in"][i], "ffn_w_out": inp["ffn_w_out"][i]})
    return ims


def layer_ret(xs, inp):
    ims = _pre_common(0, xs, inp)
    for c in range(NCORES):
        C, S = rope_tables(64, [['r'] * 128, ['c'] * 128], (c % 4) * TL, TL)
        ims[c].update({"w_in": inp["ret_w_in"][0], "ropeC": C, "ropeS": S})
    pre = _run(build_pre(0, TL, TCX), ims)
    ims = []
    for b in range(2):
        Q, Qc = _gather_batch(pre, "qT", b)
        K, Kc = _gather_batch(pre, "kT", b)
        V, Vc = _gather_batch(pre, "vT", b)
        for h in range(4):
            ims.append(ret_host_inputs(Q[h * 256:(h + 1) * 256], K[h * 256:(h + 1) * 256], V[h * 512:(h + 1) * 512],
                                       Qc[h * 256:(h + 1) * 256], Kc[h * 256:(h + 1) * 256], Vc[h * 512:(h + 1) * 512],
                                       inp["ret_decay_logit"][0][:, h]))
    mix = _run(build_ret(32), ims)
    ims = _post_common(0, xs, inp, TL + TCX)
    ng = np.ascontiguousarray(inp["ret_norm_g"][0].reshape(16, 128).T).astype(np.float32)
    for c in range(NCORES):
        b, r = c // 4, c % 4
        aT = np.zeros((2048, TL + TCX), NBF)
        for h in range(4):
            on = mix[b * 4 + h]["on"].reshape(130 * 128, 512)
            aT[h * 512:(h + 1) * 512, :TL] = on[r * TL:(r + 1) * TL].T
            aT[h * 512:(h + 1) * 512, TL:] = on[SEQ + r * TCX:SEQ + (r + 1) * TCX].T
        ims[c].update({"aT": aT, "sgT": pre[c]["sgT"], "ng": ng, "w_out": inp["ret_w_out"][0]})
    post = _run(build_post(0, TL, TCX), ims)
    return [post[c]["xoT"] for c in range(NCORES)]


def ret_host_inputs(qT, kT, vT, qcT, kcT, vcT, dl2):
    N = qT.shape[1]
    NG = N // 512

    def fm(a):
        return np.ascontiguousarray(a.reshape(2, 128, NG, 512).transpose(2, 1, 0, 3).reshape(NG, 128, 1024))

    def tm(a, F):
        return np.ascontiguousarray(a.T.reshape(NG, 4, 128, F).transpose(0, 2, 1, 3).reshape(NG, 128, 4 * F))

    def fmc(a):
        return np.ascontiguousarray(a.reshape(2, 128, 256).transpose(1, 0, 2).reshape(128, 512))

    def tmc(a, F):
        return np.ascontiguousarray(a.T.reshape(2, 128, F).transpose(1, 0, 2).reshape(128, 2 * F))

    tabs, cols = ret_tables()
    return {"qT": fm(qT), "kT": fm(kT), "ktok": tm(kT, 256), "vtok": tm(vT, 512), "qcT": fmc(qcT), "kcT": fmc(kcT),
            "kctok": tmc(kcT, 256), "vctok": tmc(vcT, 512),
            "dl": np.ascontiguousarray(np.broadcast_to(np.asarray(dl2, np.float32)[None, :], (128, 2))),
            "tabs": tabs, "cols": cols}


def layer_diff(xs, inp):
    i = 1
    import math
    lam_init = 0.8 - 0.6 * math.exp(-0.3 * i)
    ims = _pre_common(i, xs, inp)
    gains = np.ascontiguousarray(np.stack([np.tile(inp["diff_q_norm_g"][0], 2), np.tile(inp["diff_k_norm_g"][0], 2)], 1))
    for c in range(NCORES):
        C, S = rope_tables(16, [(['r'] * 32 + ['c'] * 32) * 2], (c % 4) * TL, TL)
        ims[c].update({"w_in": inp["diff_w_in"][0], "ropeC": C, "ropeS": S, "gains": gains.astype(np.float32)})
    pre = _run(build_pre(1, TL, TCX), ims)
    NK = SEQ + NCTX
    kv = []
    for b in range(2):
        K, Kc = _gather_batch(pre, "kT", b)
        V, Vc = _gather_batch(pre, "vT", b)
        kT = np.ascontiguousarray(np.concatenate([Kc, K], 1).reshape(8, 128, NK))
        Vall = np.concatenate([Vc, V], 1)
        v = np.stack([_vpack(np.ascontiguousarray(Vall[h * 128:(h + 1) * 128].T), 128) for h in range(8)], 0)
        kv.append((kT, v))
    ims = []
    for c in range(NCORES):
        b = c // 4
        ims.append({"qT": np.ascontiguousarray(pre[c]["qT"].reshape(8, 128, TL + TCX)), "kT": kv[b][0], "v": kv[b][1],
                    "lam": np.ascontiguousarray(inp["diff_lambda"][0]).astype(np.float32)})
    allk = [(t, None) for t in range(NK // 128)]
    qtiles = [(q0, 512, allk) for q0 in range(0, TL, 512)] + [(TL, TCX, [(0, None), (1, None)])]
    mix = _run(build_att(8, 128, TL + TCX, NK, 128, 2, qtiles, 64 ** -0.5, lam_init=lam_init), ims)
    ims = _post_common(i, xs, inp, TL + TCX)
    for c in range(NCORES):
        ims[c].update({"aT": np.ascontiguousarray(mix[c]["oT"].reshape(1024, TL + TCX)),
                       "sg": np.ascontiguousarray(inp["diff_subln_g"][0].reshape(128, 1)).astype(np.float32),
                       "w_out": inp["diff_w_out"][0]})
    post = _run(build_post(1, TL, TCX, lam_init), ims)
    return [post[c]["xoT"] for c in range(NCORES)]


def layer_na(xs, inp):
    i = 2
    ims = _pre_common(i, xs, inp)
    gains = np.ascontiguousarray(np.stack([np.tile(inp["na_q_norm_g"][0], 2), np.tile(inp["na_k_norm_g"][0], 2)], 1))
    for c in range(NCORES):
        ims[c].update({"w_in": inp["na_w_in"][0], "gains": gains.astype(np.float32)})
    pre = _run(build_pre(2, TL, TCX), ims)
    bias0 = na_bias_table(np.asarray(inp["na_rel_bias"][0], np.float32))
    NKL = 72 * 64
    NK = NCTX + NKL
    ims = []
    for b in range(2):
        K, Kc = _gather_batch(pre, "kT", b)
        V, Vc = _gather_batch(pre, "vT", b)
        for r in range(4):
            rows = np.clip(np.arange(64 * r - 4, 64 * r + 68), 0, 255)
            tok = (rows[:, None] * 64 + np.arange(64)[None, :]).reshape(-1)
            kT = np.ascontiguousarray(np.concatenate([Kc, K[:, tok]], 1).reshape(16, 64, NK))
            Vall = np.concatenate([Vc, V[:, tok]], 1)
            v = np.stack([_vpack(np.ascontiguousarray(Vall[h * 64:(h + 1) * 64].T), 64) for h in range(16)], 0)
            bias = bias0.copy()
            if r != 0:
                bias[0:6] = bias0[6:12]
            if r != 3:
                bias[12:18] = bias0[6:12]
            ims.append({"qT": np.ascontiguousarray(pre[b * 4 + r]["qT"].reshape(16, 64, TL + TCX)), "kT": kT, "v": v,
                        "bias": bias})
    qtiles = []
    for jj in range(16):
        slot = 0 if jj == 0 else (2 if jj == 15 else 1)
        ktl = [(0, None), (1, None)] + [(2 + 2 * jj + t, slot * 6 + t) for t in range(6)]
        qtiles.append((jj * 256, 256, ktl))
    qtiles.append((TL, TCX, [(0, None), (1, None)]))
    mix = _run(build_att(16, 64, TL + TCX, NK, 64, 1, qtiles, 64 ** -0.5, bias_shape=[18, 16, 128, 256]), ims)
    ims = _post_common(i, xs, inp, TL + TCX)
    for c in range(NCORES):
        ims[c].update({"aT": np.ascontiguousarray(mix[c]["oT"].reshape(1024, TL + TCX)), "w_out": inp["na_w_out"][0]})
    post = _run(build_post(2, TL, TCX), ims)
    return [post[c]["xoT"] for c in range(NCORES)]


def layer_mla(xs, inp):
    i = 3
    ims = _pre_common(i, xs, inp)
    hw = mla_host_weights(inp["mla_w_down"][0], inp["mla_w_uq"][0], inp["mla_w_ukv"][0], inp["mla_q_norm_g"][0],
                          inp["mla_kv_norm_g"][0], inp["mla_qk_norm_q"][0], inp["mla_qk_norm_k"][0])
    for c in range(NCORES):
        C, S = rope_tables(8, [(['r'] * 16 + ['c'] * 16) * 4], (c % 4) * TL, TL)
        ims[c].update(hw)
        ims[c].update({"ropeC": C, "ropeS": S})
    pre = _run(build_pre(3, TL, TCX), ims)
    NK = SEQ + NCTX
    kv = []
    for b in range(2):
        Kn, Knc = _gather_batch(pre, "knT", b)
        Kr, Krc = _gather_batch(pre, "krT", b)
        V, Vc = _gather_batch(pre, "vT", b)
        KnA = np.concatenate([Knc, Kn], 1).reshape(16, 64, NK)
        KrA = np.concatenate([Krc, Kr], 1)
        kT = np.ascontiguousarray(np.concatenate([KnA, np.broadcast_to(KrA[None], (16, 32, NK))], 1))
        Vall = np.concatenate([Vc, V], 1)
        v = np.stack([_vpack(np.ascontiguousarray(Vall[h * 64:(h + 1) * 64].T), 64) for h in range(16)], 0)
        kv.append((kT, v))
    ims = []
    for c in range(NCORES):
        b = c // 4
        qn = pre[c]["qnT"][:, :TL].reshape(16, 64, TL)
        qr = pre[c]["qrT"][:, :TL].reshape(16, 32, TL)
        ims.append({"qT": np.ascontiguousarray(np.concatenate([qn, qr], 1)), "kT": kv[b][0], "v": kv[b][1]})
    allk = [(t, None) for t in range(NK // 128)]
    qtiles = [(q0, 512, allk) for q0 in range(0, TL, 512)]
    mix = _run(build_att(16, 96, TL, NK, 64, 1, qtiles, 96 ** -0.5), ims)
    ims = _post_common(i, xs, inp, TL)
    for c in range(NCORES):
        ims[c].update({"aT": np.ascontiguousarray(mix[c]["oT"].reshape(1024, TL)), "w_out": inp["mla_w_out"][0]})
    post = _run(build_post(3, TL, 0), ims)
    return [post[c]["xoT"] for c in range(NCORES)]


def _split_state(x_lat, x_ctx):
    xs = []
    for c in range(NCORES):
        b, r = c // 4, c % 4
        xs.append(np.ascontiguousarray(np.concatenate([x_lat[b, r * TL:(r + 1) * TL].T,
                                                       x_ctx[b, r * TCX:(r + 1) * TCX].T], 1)).astype(np.float32))
    return xs


def _join_state(xs):
    x_lat = np.zeros((2, SEQ, DM), np.float32)
    x_ctx = np.zeros((2, NCTX, DM), np.float32)
    for c in range(NCORES):
        b, r = c // 4, c % 4
        x_lat[b, r * TL:(r + 1) * TL] = xs[c][:, :TL].T
        if xs[c].shape[1] > TL:
            x_ctx[b, r * TCX:(r + 1) * TCX] = xs[c][:, TL:].T
    return x_lat, x_ctx


def kernel(**inputs):
    inp = {k: np.asarray(v) for k, v in inputs.items()}
    xs = _split_state(inp["x"].astype(np.float32), inp["ctx"].astype(np.float32))
    for fn in (layer_ret, layer_diff, layer_na, layer_mla):
        xs = fn(xs, inp)
    return _join_state(xs)[0]
```

```python
import numpy as np
import concourse.bass as bass
import concourse.mybir as mybir
from concourse.bass_utils import run_bass_kernel_spmd

F32 = mybir.dt.float32
BF16 = mybir.dt.bfloat16
AF = mybir.ActivationFunctionType
ALU = mybir.AluOpType
AX = mybir.AxisListType

SAME_ENGINE_SYNC = True
N_DMA_SEMS = 24


class Res:
    __slots__ = ("name", "lw", "rd")

    def __init__(self, name=""):
        self.name = name
        self.lw = None
        self.rd = []


class Prog:
    ENGS = ["pe", "act", "dve", "pool", "sp"]

    def __init__(self, nc):
        self.nc = nc
        self.recs = {e: [] for e in self.ENGS}
        self.dmas = []
        self.ndma = {e: 0 for e in self.ENGS}

    def op(self, eng, fn, reads=(), writes=(), dma=False):
        deps = set()
        for r in reads:
            if r.lw is not None:
                deps.add(r.lw)
        for w in writes:
            if w.lw is not None:
                deps.add(w.lw)
            for t in w.rd:
                deps.add(t)
        idx = len(self.recs[eng])
        if dma:
            k = self.ndma[eng]
            self.ndma[eng] += 1
            tok = ("D", eng, k)
            if k >= N_DMA_SEMS:
                deps.add(("D", eng, k - N_DMA_SEMS))
        else:
            tok = (eng, idx)
        deps.discard(tok)
        rec = {"fn": fn, "deps": deps, "dma": dma, "tok": tok, "sig": False}
        self.recs[eng].append(rec)
        for r in reads:
            r.rd.append(tok)
        for w in writes:
            w.lw = tok
            w.rd = []
        return tok

    def pe(self, fn, reads=(), writes=()):
        return self.op("pe", fn, reads, writes)

    def act(self, fn, reads=(), writes=()):
        return self.op("act", fn, reads, writes)

    def dve(self, fn, reads=(), writes=()):
        return self.op("dve", fn, reads, writes)

    def pool(self, fn, reads=(), writes=()):
        return self.op("pool", fn, reads, writes)

    def dma(self, out, in_, reads=(), writes=(), q="sp", **kw):
        return self.op(q, lambda e: e.dma_start(out=out, in_=in_, **kw), reads, writes, dma=True)

    def emit(self, final_wait_tokens=()):
        nc = self.nc
        recs = self.recs
        for e in self.ENGS:
            for rec in recs[e]:
                for d in rec["deps"]:
                    if d[0] == "D":
                        continue
                    de, di = d
                    if de == e and (not SAME_ENGINE_SYNC or e in ("pe", "sp")):
                        continue
                    recs[de][di]["sig"] = True
        for t in final_wait_tokens:
            if t[0] != "D":
                recs[t[0]][t[1]]["sig"] = True
        cnt = {}
        for e in self.ENGS:
            c = 0
            for i, rec in enumerate(recs[e]):
                if rec["sig"] and not rec["dma"]:
                    c += 1
                cnt[(e, i)] = c
        import contextlib
        with contextlib.ExitStack() as st:
            esem = {e: st.enter_context(nc.semaphore("s_" + e)) for e in self.ENGS}
            dsem = {}
            for e in self.ENGS:
                if self.ndma[e] > 0:
                    dsem[e] = [st.enter_context(nc.semaphore("d_%s_%d" % (e, j)))
                               for j in range(min(N_DMA_SEMS, self.ndma[e]))]
            block = st.enter_context(nc.Block())

            def sem_of(tok):
                if tok[0] == "D":
                    _, q, k = tok
                    return dsem[q][k % N_DMA_SEMS], 16 * (k // N_DMA_SEMS + 1)
                return esem[tok[0]], cnt[tok]

            def make(e):
                def body(eng):
                    water = {}
                    for i, rec in enumerate(recs[e]):
                        need = {}
                        for d in rec["deps"]:
                            if d[0] != "D":
                                if d[0] == e and (not SAME_ENGINE_SYNC or e in ("pe", "sp")):
                                    continue
                            s, v = sem_of(d)
                            key = id(s)
                            if water.get(key, 0) >= v:
                                continue
                            if key not in need or need[key][1] < v:
                                need[key] = (s, v)
                        for key, (s, v) in need.items():
                            eng.wait_ge(s, v)
                            water[key] = v
                        ins = rec["fn"](eng)
                        if rec["dma"]:
                            s, v = sem_of(rec["tok"])
                            ins.then_inc(s, 16)
                        elif rec["sig"]:
                            ins.then_inc(esem[e], 1)
                    if e == "sp":
                        for t in final_wait_tokens:
                            s, v = sem_of(t)
                            eng.wait_ge(s, v)
                return body

            regs = {"pe": block.tensor, "act": block.scalar, "dve": block.vector,
                    "pool": block.gpsimd, "sp": block.sync}
            for e in self.ENGS:
                if recs[e] or e == "sp":
                    regs[e](make(e))

import contextlib
import ml_dtypes

NBF = ml_dtypes.bfloat16


class KB:
    def __init__(self):
        self.nc = bass.Bass("TRN2", target_bir_lowering=False)
        self.st = contextlib.ExitStack()
        self.P = Prog(self.nc)
        self.n = 0

    def din(self, name, shape, dt):
        return self.nc.dram_tensor(name, list(shape), dt, kind="ExternalInput").ap()

    def dout(self, name, shape, dt):
        return self.nc.dram_tensor(name, list(shape), dt, kind="ExternalOutput").ap()

    def sb(self, shape, dt, name=None):
        self.n += 1
        return self.st.enter_context(self.nc.sbuf_tensor(name or "sb%d" % self.n, list(shape), dt))

    def ps(self, name=None, shape=(128, 512), dt=F32):
        self.n += 1
        return self.st.enter_context(self.nc.psum_tensor(name or "ps%d" % self.n, list(shape), dt))

    def finish(self, toks):
        self.P.emit(toks)
        self.st.close()
        return self.nc

    def next_phase(self, toks):
        self.P.emit(toks)
        self.st.close()
        self.nc.all_engine_barrier()
        self.st = contextlib.ExitStack()
        self.P = Prog(self.nc)


def run_spmd(nc, in_maps):
    res = run_bass_kernel_spmd(nc, in_maps, core_ids=list(range(len(in_maps))))
    return res.results


def build_att(NH, DQ, NQ, NK, DV, maps, qtiles, scale, lam_init=None, bias_shape=None, aug=False, out_dt=F32, kshared=0):
    kb = KB()
    nc, P = kb.nc, kb.P
    NT = NK // 128
    DVa = 128 if aug else DV
    assert not aug or (DV == 64 and maps == 1)
    qT = kb.din("qT", [NH, DQ, NQ], BF16)
    kT = kb.din("kT", [NH, DQ - kshared, NK], BF16)
    if kshared:
        kS = kb.din("kS", [kshared, NK], BF16)
    v = kb.din("v", [NH, 128, NT * DV], BF16)
    oT = kb.dout("oT", [NH, DV, NQ], out_dt)
    if bias_shape is not None:
        bias = kb.din("bias", bias_shape, F32)
    if maps == 2:
        lam = kb.din("lam", [4, 64], F32)
    NB = 2
    kt_sb = [kb.sb([DQ, NK], BF16) for _ in range(NB)]
    v_sb = [kb.sb([128, NT * DVa], BF16) for _ in range(NB)]
    r_kv = [Res() for _ in range(NB)]
    q_sb = [kb.sb([DQ, 512], BF16) for _ in range(2)]
    r_q = [Res() for _ in range(2)]
    ones = kb.sb([128, 128], BF16)
    r_ones = Res()
    P.pool(lambda e: e.memset(ones[:], 1.0), [], [r_ones])
    NPB = 3
    p_sb = [kb.sb([128, 2, 512], BF16) for _ in range(NPB)]
    r_p = [Res() for _ in range(NPB)]
    s_ps = [kb.ps(shape=(128, 1024)) for _ in range(2)]
    r_s = [Res() for _ in range(2)]
    o_ps = [kb.ps() for _ in range(maps)]
    r_o = [Res() for _ in range(maps)]
    if not aug:
        l_ps = [kb.ps() for _ in range(maps)]
        r_l = [Res() for _ in range(maps)]
    rec_sb = [kb.sb([128, 512], F32) for _ in range(maps)]
    r_rec = [Res() for _ in range(maps)]
    t_sb = [kb.sb([128, 512], F32) for _ in range(maps)]
    r_t = [Res() for _ in range(maps)]
    out_sb = [kb.sb([128, 512], out_dt) for _ in range(2)]
    r_out = [Res() for _ in range(2)]
    if bias_shape is not None:
        b_sb = [kb.sb([128, 512], F32) for _ in range(4)]
        r_b = [Res() for _ in range(4)]
        sb_sb = [kb.sb([128, 2, 512], F32) for _ in range(2)]
        r_sb = [Res() for _ in range(2)]
    if maps == 2:
        lv = kb.sb([128, 256], F32)
        r_lv = Res()
        neglam = kb.sb([128, 1], F32)
        P.dma(lv[:], lam.rearrange("a b -> (a b)").partition_broadcast(128), writes=[r_lv])
        pr = kb.sb([128, 128], F32)
        sm = kb.sb([128, 2], F32)
        P.dve(lambda e: e.tensor_tensor(out=pr[:, 0:64], in0=lv[:, 0:64], in1=lv[:, 64:128], op=ALU.mult), [r_lv], [r_lv])
        P.dve(lambda e: e.tensor_tensor(out=pr[:, 64:128], in0=lv[:, 128:192], in1=lv[:, 192:256], op=ALU.mult), [r_lv], [r_lv])
        P.dve(lambda e: e.reduce_sum(out=sm[:, 0:1], in_=pr[:, 0:64], axis=AX.X), [r_lv], [r_lv])
        P.dve(lambda e: e.reduce_sum(out=sm[:, 1:2], in_=pr[:, 64:128], axis=AX.X), [r_lv], [r_lv])
        P.act(lambda e: e.activation(out=sm[:], in_=sm[:], func=AF.Exp), [r_lv], [r_lv])
        P.dve(lambda e: e.scalar_tensor_tensor(out=neglam[:], in0=sm[:, 1:2], scalar=-float(lam_init),
                                               in1=sm[:, 0:1], op0=ALU.add, op1=ALU.subtract), [r_lv], [r_lv])
    out_toks = []
    if aug:
        for hb_ in range(NB):
            P.pool(lambda e, hb_=hb_: e.memset(v_sb[hb_].rearrange("p (t d) -> p t d", d=DVa)[:, :, DV:DVa], 1.0),
                   [], [r_kv[hb_]])
    cnt = {"q": 0, "s": 0, "p": 0, "o": 0, "b": 0, "sb": 0}
    rows = [(0, DQ)] if maps == 1 else [(0, 64), (64, 128)]
    for h in range(NH):
        hb = h % NB
        P.dma(kt_sb[hb][0:DQ - kshared, :], kT[h], writes=[r_kv[hb]])
        if kshared:
            P.dma(kt_sb[hb][DQ - kshared:DQ, :], kS[:, :], writes=[r_kv[hb]])
        if aug:
            P.dma(v_sb[hb].rearrange("p (t d) -> p t d", d=DVa)[:, :, 0:DV],
                  v[h].rearrange("p (t d) -> p t d", d=DV), writes=[r_kv[hb]])
        else:
            P.dma(v_sb[hb][:], v[h], writes=[r_kv[hb]], reads=[])
        def do_qtile(h, hb, q0, nq, ktl):
            qb = cnt["q"] % 2
            cnt["q"] += 1
            P.dma(q_sb[qb][:, :nq], qT[h, :, q0:q0 + nq], writes=[r_q[qb]])
            nkt = len(ktl)
            groups = []
            if maps == 2:
                for ki, (kt, bi) in enumerate(ktl):
                    groups.append([(kt, bi, 0, ki == 0, ki == nkt - 1), (kt, bi, 1, ki == 0, ki == nkt - 1)])
            else:
                ki = 0
                while ki < nkt:
                    kt, bi = ktl[ki]
                    g = [(kt, bi, 0, ki == 0, ki == nkt - 1)]
                    if ki + 1 < nkt and ((ktl[ki + 1][1] is None) == (bi is None)):
                        kt2, bi2 = ktl[ki + 1]
                        g.append((kt2, bi2, 0, False, ki + 1 == nkt - 1))
                        ki += 1
                    groups.append(g)
                    ki += 1
            gslots = []
            for g in groups:
                gslots.append((cnt["s"] % 2, cnt["p"] % NPB))
                cnt["s"] += 1
                cnt["p"] += 1

            def emit_qk(gi):
                sbi, pbi = gslots[gi]
                for u, (kt, bi, m, st_, sp2_) in enumerate(groups[gi]):
                    r0, r1 = rows[m]
                    P.pe(lambda e, sbi=sbi, u=u, r0=r0, r1=r1, kt=kt: e.matmul(
                        s_ps[sbi][:, u * 512:u * 512 + nq], lhsT=kt_sb[hb][r0:r1, kt * 128:(kt + 1) * 128],
                        rhs=q_sb[qb][r0:r1, :nq], start=True, stop=True),
                        [r_kv[hb], r_q[qb]], [r_s[sbi]])

            def emit_rest(gi):
                sbi, pbi = gslots[gi]
                g = groups[gi]
                nu = len(g)
                s3 = s_ps[sbi].rearrange("p (u n) -> p u n", u=2)
                if g[0][1] is not None:
                    s2 = cnt["sb"] % 2
                    cnt["sb"] += 1
                    for u, (kt, bi, m, st_, sp2_) in enumerate(g):
                        bb = cnt["b"] % 4
                        cnt["b"] += 1
                        P.dma(b_sb[bb][:, :nq], bias[bi, h], writes=[r_b[bb]])
                        P.dve(lambda e, u=u, bb=bb, s2=s2: e.scalar_tensor_tensor(
                            out=sb_sb[s2][:, u, :nq], in0=s_ps[sbi][:, u * 512:u * 512 + nq], scalar=float(scale),
                            in1=b_sb[bb][:, :nq], op0=ALU.mult, op1=ALU.add), [r_s[sbi], r_b[bb]], [r_sb[s2]])
                    P.act(lambda e, s2=s2: e.activation(
                        out=p_sb[pbi][:, :nu, :nq], in_=sb_sb[s2][:, :nu, :nq], func=AF.Exp), [r_sb[s2]], [r_p[pbi]])
                else:
                    P.act(lambda e: e.activation(
                        out=p_sb[pbi][:, :nu, :nq], in_=s3[:, :nu, :nq], func=AF.Exp, scale=float(scale)),
                        [r_s[sbi]], [r_p[pbi]])
                for u, (kt, bi, m, st_, sp2_) in enumerate(g):
                    P.pe(lambda e, u=u, m=m, kt=kt, st_=st_, sp2_=sp2_: e.matmul(
                        o_ps[m][:DVa, :nq], lhsT=v_sb[hb][:, kt * DVa:(kt + 1) * DVa], rhs=p_sb[pbi][:, u, :nq],
                        start=st_, stop=sp2_), [r_kv[hb], r_p[pbi]], [r_o[m]])
                    if not aug:
                        P.pe(lambda e, u=u, m=m, st_=st_, sp2_=sp2_: e.matmul(
                            l_ps[m][:, :nq], lhsT=ones[:], rhs=p_sb[pbi][:, u, :nq],
                            start=st_, stop=sp2_), [r_ones, r_p[pbi]], [r_l[m]])

            emit_qk(0)
            for gi in range(len(groups)):
                if gi + 1 < len(groups):
                    emit_qk(gi + 1)
                emit_rest(gi)
            ob = cnt["o"] % 2
            cnt["o"] += 1
            for m in range(maps):
                if aug:
                    P.dve(lambda e, m=m, nq=nq: e.reciprocal(out=rec_sb[m][:DV, :nq], in_=o_ps[m][DV:2 * DV, :nq]),
                          [r_o[m]], [r_rec[m]])
                else:
                    P.dve(lambda e, m=m, nq=nq: e.reciprocal(out=rec_sb[m][:DV, :nq], in_=l_ps[m][:DV, :nq]),
                          [r_l[m]], [r_rec[m]])
                dst = out_sb[ob] if (maps == 1) else t_sb[m]
                rdst = r_out[ob] if (maps == 1) else r_t[m]
                P.dve(lambda e, m=m, nq=nq, dst=dst: e.tensor_tensor(
                    out=dst[:DV, :nq], in0=o_ps[m][:DV, :nq], in1=rec_sb[m][:DV, :nq], op=ALU.mult),
                    [r_o[m], r_rec[m]], [rdst])
            if maps == 2:
                P.dve(lambda e, nq=nq, ob=ob: e.scalar_tensor_tensor(
                    out=out_sb[ob][:DV, :nq], in0=t_sb[1][:DV, :nq], scalar=neglam[:DV, 0:1], in1=t_sb[0][:DV, :nq],
                    op0=ALU.mult, op1=ALU.add), [r_t[0], r_t[1], r_lv], [r_out[ob]])
            out_toks.append(P.dma(oT[h, :, q0:q0 + nq], out_sb[ob][:DV, :nq], reads=[r_out[ob]]))
        for (q0_, nq_, ktl_) in qtiles:
            do_qtile(h, hb, q0_, nq_, ktl_)
    return kb.finish(out_toks[-N_DMA_SEMS:])


EPS = 1e-6


class TB:
    def __init__(self, t, r=None):
        self.t = t
        self.r = r or Res()


class Ring:
    def __init__(self, kb, n, shape=None, dt=None, psum=False):
        self.items = [TB(kb.ps() if psum else kb.sb(shape, dt)) for _ in range(n)]
        self.i = 0

    def next(self):
        b = self.items[self.i % len(self.items)]
        self.i += 1
        return b


def host_consts():
    c = np.zeros((6, 128, 128), np.float32)
    c[0] = 1.0
    for g in range(2):
        c[1, g * 64:(g + 1) * 64, g * 64:(g + 1) * 64] = 1.0
    for g in range(4):
        c[2, g * 32:(g + 1) * 32, g * 32:(g + 1) * 32] = 1.0
    for idx, q in ((3, 64), (4, 16), (5, 8)):
        for m in range(128):
            src = m + q if (m % (2 * q)) < q else m - q
            c[idx, src, m] = 1.0
    return np.ascontiguousarray(c.transpose(1, 0, 2).reshape(128, 6 * 128)).astype(NBF)


def rope_tables(q, kind_cols, pos0, npos):
    pos = np.arange(pos0, pos0 + npos)
    row = (pos // 64).astype(np.float32)
    col = (pos % 64).astype(np.float32)
    inv = np.power(np.float32(10000.0), -np.arange(q, dtype=np.float32) / q).astype(np.float32)
    C = np.zeros((len(kind_cols), 128, npos), np.float32)
    S = np.zeros((len(kind_cols), 128, npos), np.float32)
    for t, kc in enumerate(kind_cols):
        for p in range(128):
            pp = row if kc[p] == 'r' else col
            ang = (pp * inv[p % q]).astype(np.float32)
            C[t, p] = np.cos(ang)
            sgn = -1.0 if (p % (2 * q)) < q else 1.0
            S[t, p] = sgn * np.sin(ang)
    return C, S


class TokCommon:
    def __init__(self, kb, T, consts_ap=None):
        self.kb = kb
        self.P = kb.P
        self.T = T
        P = self.P
        self.consts = consts_ap if consts_ap is not None else kb.din("consts", [128, 6 * 128], BF16)
        self.c_sb = TB(kb.sb([128, 6 * 128], BF16))
        P.dma(self.c_sb.t[:], self.consts[:, :], writes=[self.c_sb.r])
        self.stage = Ring(kb, 2, [128, 1024], F32)
        self.sq = Ring(kb, 3, [128, 512], BF16)
        self.ps_ss = Ring(kb, 2, psum=True)
        self.rstd = Ring(kb, 2, [128, 512], F32)
        self.tmp = Ring(kb, 3, [128, 512], F32)
        self.weng = 0
        self.epscol = {32: 0, 64: 1, 128: 2, 256: 3, 1024: 4, 512: 5}
        self.epsb = TB(kb.sb([128, 6], F32))
        for nf, col in self.epscol.items():
            P.pool(lambda e, nf=nf, col=col: e.memset(self.epsb.t[:, col:col + 1], float(nf * EPS)), [], [self.epsb.r])

    def cmat(self, i):
        return self.c_sb.t[:, i * 128:(i + 1) * 128]

    def load_w(self, w_ap, K, N, name=None):
        kb, P = self.kb, self.P
        KC = (K + 127) // 128
        wb = TB(kb.sb([128, KC, N], BF16))
        for k in range(KC):
            kr = min(128, K - k * 128)
            for c0 in range(0, N, 1024):
                cn = min(1024, N - c0)
                stg = self.stage.next()
                P.dma(stg.t[:kr, :cn], w_ap[k * 128:k * 128 + kr, c0:c0 + cn], writes=[stg.r])
                eng = "pool" if (self.weng % 2 == 0) else "dve"
                self.weng += 1
                P.op(eng, lambda e, stg=stg, k=k, c0=c0, cn=cn, kr=kr: e.tensor_copy(
                    out=wb.t[:kr, k, c0:c0 + cn], in_=stg.t[:kr, :cn]), [stg.r], [wb.r])
        return wb

    def modvecs(self, mod_ap, nch, plus1):
        kb, P = self.kb, self.P
        mod = TB(kb.sb([128, nch, 2], F32))
        P.dma(mod.t[:], mod_ap, writes=[mod.r])
        if plus1:
            lo, hi = min(plus1), max(plus1) + 1
            P.dve(lambda e: e.tensor_scalar(out=mod.t[:, lo:hi, :], in0=mod.t[:, lo:hi, :], scalar1=1.0, scalar2=None,
                                            op0=ALU.add), [mod.r], [mod.r])
        return mod

    def sumsq_rstd(self, srcs, n, blockmat, ntok, nfeat, src_res):
        P = self.P
        ps = self.ps_ss.next()
        for i, s in enumerate(srcs):
            sq = self.sq.next()
            P.act(lambda e, sq=sq, s=s: e.activation(out=sq.t[:, :ntok], in_=s, func=AF.Square), src_res, [sq.r])
            P.pe(lambda e, ps=ps, sq=sq, i=i: e.matmul(ps.t[:, :ntok], lhsT=blockmat, rhs=sq.t[:, :ntok],
                                                      start=(i == 0), stop=(i == len(srcs) - 1)),
                 [sq.r, self.c_sb.r], [ps.r])
        r = self.rstd.next()
        self.rsqrt(r, ps, 128, ntok, nfeat)
        return r

    def rsqrt(self, r, ps, np_, ntok, nfeat):
        P = self.P
        P.act(lambda e: e.activation(out=r.t[:np_, :ntok], in_=ps.t[:np_, :ntok], func=AF.Sqrt,
                                     bias=self.epsb.t[:np_, self.epscol[nfeat]:self.epscol[nfeat] + 1], scale=1.0),
              [ps.r, self.epsb.r], [r.r])
        P.dve(lambda e: e.reciprocal(out=r.t[:np_, :ntok], in_=r.t[:np_, :ntok]), [r.r], [r.r])

    def modulate(self, xs, mod, sh0, sc0, j, h, ntok):
        P = self.P
        r = self.sumsq_rstd([xs.t[:, k, :ntok] for k in range(8)], 8, self.cmat(0), ntok, 1024, [xs.r])
        for k in range(8):
            t = self.tmp.next()
            P.dve(lambda e, t=t, k=k, r=r: e.scalar_tensor_tensor(
                out=t.t[:, :ntok], in0=xs.t[:, k, :ntok], scalar=32.0, in1=r.t[:, :ntok],
                op0=ALU.mult, op1=ALU.mult), [xs.r, r.r], [t.r])
            P.act(lambda e, t=t, k=k: e.activation(
                out=h.t[:, k, :ntok], in_=t.t[:, :ntok], func=AF.Identity,
                scale=mod.t[:, sc0 + k, j:j + 1], bias=mod.t[:, sh0 + k, j:j + 1]), [t.r, mod.r], [h.r])


def build_pre(kind, T_lat, T_ctx):
    kb = KB()
    P = kb.P
    T = T_lat + T_ctx
    tc = TokCommon(kb, T)
    xT = kb.din("xT", [1024, T], F32)
    modin = kb.din("mod", [128, 16, 2], F32)
    mod = tc.modvecs(modin[:, :, :], 16, set(range(8, 16)))
    outs = {}
    ngain = {0: 0, 1: 2, 2: 2, 3: 7}[kind]
    gsc = {1: [8.0, 8.0], 2: [8.0, 8.0], 3: [16.0, 16.0, 128.0 ** 0.5, 32.0 ** 0.5, 8.0, 32.0 ** 0.5, 8.0]}.get(kind, [])
    if ngain:
        gains = kb.din("gains", [128, ngain], F32)
        g_sb = TB(kb.sb([128, ngain], F32))
        P.dma(g_sb.t[:], gains[:, :], writes=[g_sb.r])
        for i, s in enumerate(gsc):
            P.dve(lambda e, i=i, s=s: e.tensor_scalar(out=g_sb.t[:, i:i + 1], in0=g_sb.t[:, i:i + 1], scalar1=float(s),
                                                      scalar2=None, op0=ALU.mult), [g_sb.r], [g_sb.r])
    nrt = {0: 2, 1: 1, 2: 0, 3: 1}[kind]
    if nrt:
        ropeC = kb.din("ropeC", [nrt, 128, T_lat], F32)
        ropeS = kb.din("ropeS", [nrt, 128, T_lat], F32)
        rC = Ring(kb, 2, [128, nrt, 512], F32)
        rS = Ring(kb, 2, [128, nrt, 512], F32)
    if kind == 0:
        w = tc.load_w(kb.din("w_in", [1024, 6144], F32), 1024, 6144)
        for nm, F in (("qT", 1024), ("kT", 1024), ("vT", 2048), ("sgT", 2048)):
            outs[nm] = kb.dout(nm, [F, T], BF16)
    elif kind in (1, 2):
        w = tc.load_w(kb.din("w_in", [1024, 3072], F32), 1024, 3072)
        for nm in ("qT", "kT", "vT"):
            outs[nm] = kb.dout(nm, [1024, T], BF16)
    else:
        w = tc.load_w(kb.din("w_down", [1024, 416], F32), 1024, 416)
        w_uq = tc.load_w(kb.din("w_uq", [256, 1536], F32), 256, 1536)
        w_ukv = tc.load_w(kb.din("w_ukv", [128, 2048], F32), 128, 2048)
        for nm, F in (("qnT", 1024), ("qrT", 512), ("knT", 1024), ("krT", 32), ("vT", 1024)):
            outs[nm] = kb.dout(nm, [F, T], BF16)
    xs_r = Ring(kb, 2, [128, 8, 512], F32)
    h_r = Ring(kb, 2, [128, 8, 512], BF16)
    ps_r = Ring(kb, 3, psum=True)
    ps_x = Ring(kb, 2, psum=True)
    xb_r = Ring(kb, 2, [128, 512], BF16)
    ob_r = Ring(kb, 3, [128, 512], BF16)
    if kind == 3:
        ql_r = Ring(kb, 2, [128, 2, 512], BF16)
        ckv_r = Ring(kb, 2, [128, 512], BF16)
    toks = []
    tiles = [(c0, 512, 0) for c0 in range(0, T_lat, 512)] + ([(T_lat, T_ctx, 1)] if T_ctx else [])

    def proj(wt, kcs, col0, ncols, src, ntok):
        ps = ps_r.next()
        for i, k in enumerate(kcs):
            P.pe(lambda e, ps=ps, k=k, i=i: e.matmul(
                ps.t[:ncols, :ntok], lhsT=wt.t[:, k, col0:col0 + ncols], rhs=src.t[:, k, :ntok],
                start=(i == 0), stop=(i == len(kcs) - 1)), [wt.r, src.r], [ps.r])
        return ps

    def do_tile(c0, ntok, j):
        lat = (j == 0)
        xs = xs_r.next()
        P.dma(xs.t[:, :, :ntok], xT[:, c0:c0 + ntok].rearrange("(k p) t -> p k t", p=128), writes=[xs.r])
        if nrt and lat:
            cC, cS = rC.next(), rS.next()
            P.dma(cC.t[:], ropeC[:, :, c0:c0 + ntok].rearrange("n p t -> p n t"), writes=[cC.r])
            P.dma(cS.t[:], ropeS[:, :, c0:c0 + ntok].rearrange("n p t -> p n t"), writes=[cS.r])
        h = h_r.next()
        tc.modulate(xs, mod, 0, 8, j, h, ntok)

        def finish(ps, np_, dst, gn=None, pre_scale=None, rope=None, silu=False, keep=None):
            src = ps.t[:np_, :ntok]
            xn = None
            if gn is not None:
                bm, nfeat, gcol = gn
                r = tc.sumsq_rstd([src], 1, tc.cmat(bm)[:np_, :np_], ntok, nfeat, [ps.r]) if np_ == 128 else None
                if r is None:
                    r = tc.rstd.next()
                    sq = tc.sq.next()
                    pss = tc.ps_ss.next()
                    P.act(lambda e: e.activation(out=sq.t[:np_, :ntok], in_=src, func=AF.Square), [ps.r], [sq.r])
                    P.pe(lambda e: e.matmul(pss.t[:np_, :ntok], lhsT=tc.cmat(bm)[:np_, :np_], rhs=sq.t[:np_, :ntok],
                                            start=True, stop=True), [sq.r, tc.c_sb.r], [pss.r])
                    tc.rsqrt(r, pss, np_, ntok, nfeat)
                xn = tc.tmp.next()
                P.dve(lambda e: e.scalar_tensor_tensor(
                    out=xn.t[:np_, :ntok], in0=src, scalar=g_sb.t[:np_, gcol:gcol + 1], in1=r.t[:np_, :ntok],
                    op0=ALU.mult, op1=ALU.mult), [ps.r, r.r, g_sb.r], [xn.r])
            elif (rope is not None and lat) or pre_scale is not None:
                xn = tc.tmp.next()
                P.act(lambda e: e.activation(out=xn.t[:np_, :ntok], in_=src, func=AF.Identity,
                                             scale=float(pre_scale or 1.0)), [ps.r], [xn.r])
            if keep is not None:
                ob, oap = keep
            else:
                ob = ob_r.next()
                oap = ob.t[:np_, :ntok]
            if rope is not None and lat:
                pm, ty = rope
                xb = xb_r.next()
                P.act(lambda e: e.activation(out=xb.t[:np_, :ntok], in_=xn.t[:np_, :ntok], func=AF.Copy), [xn.r], [xb.r])
                px = ps_x.next()
                P.pe(lambda e: e.matmul(px.t[:np_, :ntok], lhsT=tc.cmat(pm)[:np_, :np_], rhs=xb.t[:np_, :ntok],
                                        start=True, stop=True), [xb.r, tc.c_sb.r], [px.r])
                t1 = tc.tmp.next()
                t2 = tc.tmp.next()
                P.pool(lambda e: e.tensor_tensor(out=t1.t[:np_, :ntok], in0=xn.t[:np_, :ntok],
                                                 in1=cC.t[:np_, ty, :ntok], op=ALU.mult), [xn.r, cC.r], [t1.r])
                P.dve(lambda e: e.tensor_tensor(out=t2.t[:np_, :ntok], in0=px.t[:np_, :ntok],
                                                in1=cS.t[:np_, ty, :ntok], op=ALU.mult), [px.r, cS.r], [t2.r])
                P.pool(lambda e: e.tensor_tensor(out=oap, in0=t1.t[:np_, :ntok], in1=t2.t[:np_, :ntok], op=ALU.add),
                       [t1.r, t2.r], [ob.r])
            elif xn is not None:
                P.pool(lambda e: e.tensor_copy(out=oap, in_=xn.t[:np_, :ntok]), [xn.r], [ob.r])
            else:
                P.act(lambda e: e.activation(out=oap, in_=src, func=(AF.Silu if silu else AF.Copy)), [ps.r], [ob.r])
            if keep is None:
                toks.append(P.dma(dst, oap, reads=[ob.r]))

        K8 = list(range(8))
        if kind == 0:
            for m in range(48):
                ps = proj(w, K8, m * 128, 128, h, ntok)
                if m < 8:
                    finish(ps, 128, outs["qT"][m * 128:(m + 1) * 128, c0:c0 + ntok], rope=(3, m % 2))
                elif m < 16:
                    finish(ps, 128, outs["kT"][(m - 8) * 128:(m - 7) * 128, c0:c0 + ntok], rope=(3, m % 2),
                           pre_scale=1.0 / 16.0)
                elif m < 32:
                    finish(ps, 128, outs["vT"][(m - 16) * 128:(m - 15) * 128, c0:c0 + ntok])
                else:
                    finish(ps, 128, outs["sgT"][(m - 32) * 128:(m - 31) * 128, c0:c0 + ntok], silu=True)
        elif kind in (1, 2):
            for m in range(24):
                ps = proj(w, K8, m * 128, 128, h, ntok)
                nm = ("qT", "kT", "vT")[m // 8]
                dst = outs[nm][(m % 8) * 128:(m % 8 + 1) * 128, c0:c0 + ntok]
                if m < 16:
                    finish(ps, 128, dst, gn=(1, 64, m // 8), rope=((4, 0) if kind == 1 else None))
                else:
                    finish(ps, 128, dst)
        else:
            p0 = proj(w, K8, 0, 128, h, ntok)
            p1 = proj(w, K8, 128, 128, h, ntok)
            r = tc.sumsq_rstd([p0.t[:, :ntok], p1.t[:, :ntok]], 2, tc.cmat(0), ntok, 256, [p0.r, p1.r])
            ql = ql_r.next()
            for k, pp in enumerate((p0, p1)):
                P.dve(lambda e, k=k, pp=pp: e.scalar_tensor_tensor(
                    out=ql.t[:, k, :ntok], in0=pp.t[:, :ntok], scalar=g_sb.t[:, k:k + 1], in1=r.t[:, :ntok],
                    op0=ALU.mult, op1=ALU.mult), [pp.r, r.r, g_sb.r], [ql.r])
            p2 = proj(w, K8, 256, 128, h, ntok)
            ckv = ckv_r.next()
            finish(p2, 128, None, gn=(0, 128, 2), keep=(ckv, ckv.t[:, :ntok]))
            p3 = proj(w, K8, 384, 32, h, ntok)
            finish(p3, 32, outs["krT"][0:32, c0:c0 + ntok], gn=(2, 32, 3), rope=(5, 0))
            ckv3 = TB(ckv.t.rearrange("p (o t) -> p o t", o=1) if False else ckv.t, ckv.r)
            for m in range(12):
                ps = proj(w_uq, [0, 1], m * 128, 128, ql, ntok)
                if m < 8:
                    finish(ps, 128, outs["qnT"][m * 128:(m + 1) * 128, c0:c0 + ntok], gn=(1, 64, 4))
                else:
                    finish(ps, 128, outs["qrT"][(m - 8) * 128:(m - 7) * 128, c0:c0 + ntok], gn=(2, 32, 5), rope=(5, 0))
            for m in range(16):
                ps = ps_r.next()
                P.pe(lambda e, ps=ps, m=m: e.matmul(ps.t[:, :ntok], lhsT=w_ukv.t[:, 0, m * 128:(m + 1) * 128],
                                                   rhs=ckv.t[:, :ntok], start=True, stop=True), [w_ukv.r, ckv.r], [ps.r])
                if m < 8:
                    finish(ps, 128, outs["knT"][m * 128:(m + 1) * 128, c0:c0 + ntok], gn=(1, 64, 6))
                else:
                    finish(ps, 128, outs["vT"][(m - 8) * 128:(m - 7) * 128, c0:c0 + ntok])
    for (c0_, ntok_, j_) in tiles:
        do_tile(c0_, ntok_, j_)
    return kb.finish(toks[-N_DMA_SEMS:])


def mla_host_weights(wd, wuq, wukv, qg, kvg, gq, gk):
    cn = [h * 96 + i for h in range(16) for i in range(64)]
    cr = [h * 96 + 64 + i for h in range(16) for i in range(32)]
    kn = [h * 128 + i for h in range(16) for i in range(64)]
    vv = [h * 128 + 64 + i for h in range(16) for i in range(64)]
    gains = np.stack([qg[:128], qg[128:], kvg, np.tile(gk[64:], 4), np.tile(gq[:64], 2), np.tile(gq[64:], 4),
                      np.tile(gk[:64], 2)], 1).astype(np.float32)
    return {"w_down": np.ascontiguousarray(wd), "w_uq": np.ascontiguousarray(wuq[:, cn + cr]),
            "w_ukv": np.ascontiguousarray(wukv[:, kn + vv]), "gains": np.ascontiguousarray(gains)}


def build_post(kind, T_lat, T_ctx, lam_init=0.0):
    kb = KB()
    T = T_lat + T_ctx
    Fa = 2048 if kind == 0 else 1024
    KA = Fa // 128
    xT = kb.din("xT", [1024, T], F32)
    a_dt = BF16
    aT = kb.din("aT", [Fa, T], a_dt)
    w_out = kb.din("w_out", [Fa, 1024], F32)
    modin = kb.din("mod", [128, 32, 2], F32)
    w1 = kb.din("ffn_w_in", [1024, 5632], F32)
    w2 = kb.din("ffn_w_out", [2816, 1024], F32)
    x1T = kb.nc.dram_tensor("x1T", [1024, T], F32, kind="Internal").ap()
    xoT = kb.dout("xoT", [1024, T], F32)
    if kind == 0:
        sgT = kb.din("sgT", [2048, T], BF16)
        ng = kb.din("ng", [128, 16], F32)
    if kind == 1:
        sg = kb.din("sg", [128, 1], F32)
    P = kb.P
    tc = TokCommon(kb, T)
    consts_ap = tc.consts
    mod = tc.modvecs(modin[:, 0:8, :], 8, set())
    wo = tc.load_w(w_out, Fa, 1024)
    if kind == 0:
        ng_sb = TB(kb.sb([128, 16], F32))
        P.dma(ng_sb.t[:], ng[:, :], writes=[ng_sb.r])
    if kind == 1:
        sg_sb = TB(kb.sb([128, 1], F32))
        P.dma(sg_sb.t[:], sg[:, :], writes=[sg_sb.r])
        P.dve(lambda e: e.tensor_scalar(out=sg_sb.t[:], in0=sg_sb.t[:], scalar1=float((1.0 - lam_init) * 128.0 ** 0.5),
                                        scalar2=None, op0=ALU.mult), [sg_sb.r], [sg_sb.r])
    xs_r = Ring(kb, 2, [128, 8, 512], F32)
    a_r = Ring(kb, 2, [128, KA, 512], a_dt)
    ab_r = Ring(kb, 2, [128, KA, 512], BF16)
    if kind == 0:
        sg_r = Ring(kb, 2, [128, KA, 512], BF16)
    ps_r = Ring(kb, 3, psum=True)
    toks = []
    tiles = [(c0, 512, 0) for c0 in range(0, T_lat, 512)] + ([(T_lat, T_ctx, 1)] if T_ctx else [])

    def tile_a(c0, ntok, j):
        xs = xs_r.next()
        P.dma(xs.t[:, :, :ntok], xT[:, c0:c0 + ntok].rearrange("(k p) t -> p k t", p=128), writes=[xs.r])
        a = a_r.next()
        P.dma(a.t[:, :, :ntok], aT[:, c0:c0 + ntok].rearrange("(k p) t -> p k t", p=128), writes=[a.r])
        ab = ab_r.next()
        if kind == 0:
            sgt = sg_r.next()
            P.dma(sgt.t[:, :, :ntok], sgT[:, c0:c0 + ntok].rearrange("(k p) t -> p k t", p=128), writes=[sgt.r])
            for k in range(KA):
                P.dve(lambda e, k=k: e.scalar_tensor_tensor(
                    out=ab.t[:, k, :ntok], in0=a.t[:, k, :ntok], scalar=ng_sb.t[:, k:k + 1], in1=sgt.t[:, k, :ntok],
                    op0=ALU.mult, op1=ALU.mult), [a.r, sgt.r, ng_sb.r], [ab.r])
        elif kind == 1:
            for k in range(KA):
                r = tc.sumsq_rstd([a.t[:, k, :ntok]], 1, tc.cmat(0), ntok, 128, [a.r])
                P.dve(lambda e, k=k, r=r: e.scalar_tensor_tensor(
                    out=ab.t[:, k, :ntok], in0=a.t[:, k, :ntok], scalar=sg_sb.t[:, 0:1], in1=r.t[:, :ntok],
                    op0=ALU.mult, op1=ALU.mult), [a.r, r.r, sg_sb.r], [ab.r])
        else:
            ab = a
        for m in range(8):
            ps = ps_r.next()
            for k in range(KA):
                P.pe(lambda e, ps=ps, k=k, m=m: e.matmul(
                    ps.t[:, :ntok], lhsT=wo.t[:, k, m * 128:(m + 1) * 128], rhs=ab.t[:, k, :ntok],
                    start=(k == 0), stop=(k == KA - 1)), [wo.r, ab.r], [ps.r])
            P.dve(lambda e, ps=ps, m=m: e.scalar_tensor_tensor(
                out=xs.t[:, m, :ntok], in0=ps.t[:, :ntok], scalar=mod.t[:, m, j:j + 1], in1=xs.t[:, m, :ntok],
                op0=ALU.mult, op1=ALU.add), [ps.r, mod.r, xs.r], [xs.r])
        toks.append(P.dma(x1T[:, c0:c0 + ntok].rearrange("(k p) t -> p k t", p=128), xs.t[:, :, :ntok], reads=[xs.r]))

    for t in tiles:
        tile_a(*t)
    kb.next_phase(toks)
    P = kb.P
    tc = TokCommon(kb, T, consts_ap)
    mod = tc.modvecs(modin[:, 8:32, :], 24, set(range(8, 16)))
    w1s = tc.load_w(w1, 1024, 5632)
    w2s = tc.load_w(w2, 2816, 1024)
    NT = 256
    xs_r = Ring(kb, 2, [128, 8, NT], F32)
    h_r = Ring(kb, 1, [128, 8, NT], BF16)
    act_r = Ring(kb, 1, [128, 22, NT], BF16)
    ps_a = Ring(kb, 2, psum=True)
    ps_g = Ring(kb, 2, psum=True)
    ps_o = Ring(kb, 2, psum=True)
    toks = []
    tiles = [(c0, NT, 0) for c0 in range(0, T_lat, NT)] + ([(T_lat, T_ctx, 1)] if T_ctx else [])

    def tile_b(c0, ntok, j):
        xs = xs_r.next()
        P.dma(xs.t[:, :, :ntok], x1T[:, c0:c0 + ntok].rearrange("(k p) t -> p k t", p=128), writes=[xs.r])
        h = h_r.next()
        tc.modulate(xs, mod, 0, 8, j, h, ntok)
        act = act_r.next()
        for jj in range(22):
            pa, pg = ps_a.next(), ps_g.next()
            for k in range(8):
                P.pe(lambda e, pa=pa, k=k, jj=jj: e.matmul(
                    pa.t[:, :ntok], lhsT=w1s.t[:, k, jj * 128:(jj + 1) * 128], rhs=h.t[:, k, :ntok],
                    start=(k == 0), stop=(k == 7)), [w1s.r, h.r], [pa.r])
            for k in range(8):
                P.pe(lambda e, pg=pg, k=k, jj=jj: e.matmul(
                    pg.t[:, :ntok], lhsT=w1s.t[:, k, 2816 + jj * 128:2816 + (jj + 1) * 128], rhs=h.t[:, k, :ntok],
                    start=(k == 0), stop=(k == 7)), [w1s.r, h.r], [pg.r])
            sa = tc.tmp.next()
            P.act(lambda e, sa=sa, pa=pa: e.activation(out=sa.t[:, :ntok], in_=pa.t[:, :ntok], func=AF.Silu),
                  [pa.r], [sa.r])
            P.dve(lambda e, sa=sa, pg=pg, jj=jj: e.tensor_tensor(
                out=act.t[:, jj, :ntok], in0=sa.t[:, :ntok], in1=pg.t[:, :ntok], op=ALU.mult), [sa.r, pg.r], [act.r])
        for m in range(8):
            ps = ps_o.next()
            for jj in range(22):
                P.pe(lambda e, ps=ps, jj=jj, m=m: e.matmul(
                    ps.t[:, :ntok], lhsT=w2s.t[:, jj, m * 128:(m + 1) * 128], rhs=act.t[:, jj, :ntok],
                    start=(jj == 0), stop=(jj == 21)), [w2s.r, act.r], [ps.r])
            P.dve(lambda e, ps=ps, m=m: e.scalar_tensor_tensor(
                out=xs.t[:, m, :ntok], in0=ps.t[:, :ntok], scalar=mod.t[:, 16 + m, j:j + 1], in1=xs.t[:, m, :ntok],
                op0=ALU.mult, op1=ALU.add), [ps.r, mod.r, xs.r], [xs.r])
        toks.append(P.dma(xoT[:, c0:c0 + ntok].rearrange("(k p) t -> p k t", p=128), xs.t[:, :, :ntok], reads=[xs.r]))

    for t in tiles:
        tile_b(*t)
    return kb.finish(toks[-N_DMA_SEMS:])


def ret_tables():
    j = np.arange(128, dtype=np.float32)[:, None]
    i = np.arange(128, dtype=np.float32)[None, :]
    t = np.zeros((8, 128, 128), np.float32)
    t[0] = np.maximum(i - j, 0)
    t[1] = (i >= j)
    t[2] = np.maximum(j - i, 0)
    t[3] = (j >= i)
    t[4] = np.broadcast_to(i + 1, (128, 128))
    t[5] = np.broadcast_to(128 - i, (128, 128))
    t[6] = 128 + i - j
    t[7] = 128 + j - i
    p = np.arange(128, dtype=np.float32)
    cols = np.stack([127 - p, p, 255 - p, 128 + p], 1)
    return (np.ascontiguousarray(t.transpose(1, 0, 2).reshape(128, 8 * 128)), np.ascontiguousarray(cols))


def build_ret(NG=32, with_ctx=True):
    kb = KB()
    P = kb.P
    NCH = NG * 4
    qT_in = kb.din("qT", [NG, 128, 1024], BF16)
    kT_in = kb.din("kT", [NG, 128, 1024], BF16)
    kt_in = kb.din("ktok", [NG, 128, 1024], BF16)
    vt_in = kb.din("vtok", [NG, 128, 2048], BF16)
    qc_in = kb.din("qcT", [128, 512], BF16)
    kc_in = kb.din("kcT", [128, 512], BF16)
    kct_in = kb.din("kctok", [128, 512], BF16)
    vct_in = kb.din("vctok", [128, 1024], BF16)
    dl_in = kb.din("dl", [128, 2], F32)
    tab_in = kb.din("tabs", [128, 1024], F32)
    col_in = kb.din("cols", [128, 4], F32)
    on_out = kb.dout("on", [NCH + 2, 128, 512], BF16)
    obD = kb.nc.dram_tensor("obD", [NCH, 128, 512], F32, kind="Internal").ap()
    r_obD = [Res() for _ in range(NCH)]

    tabs = TB(kb.sb([128, 1024], F32))
    cols = TB(kb.sb([128, 4], F32))
    lg = TB(kb.sb([128, 2], F32))
    one = TB(kb.sb([128, 1], F32))
    epsb = TB(kb.sb([128, 1], F32))
    P.dma(tabs.t[:], tab_in[:, :], writes=[tabs.r])
    P.dma(cols.t[:], col_in[:, :], writes=[cols.r])
    P.dma(lg.t[:], dl_in[:, :], writes=[lg.r])
    P.pool(lambda e: e.memset(one.t[:], 1.0), [], [one.r])
    P.pool(lambda e: e.memset(epsb.t[:], EPS), [], [epsb.r])
    P.act(lambda e: e.activation(out=lg.t[:], in_=lg.t[:], func=AF.Exp, scale=-1.0), [lg.r], [lg.r])
    P.act(lambda e: e.activation(out=lg.t[:], in_=lg.t[:], func=AF.Ln, bias=one.t[:, 0:1], scale=1.0), [lg.r, one.r], [lg.r])
    P.dve(lambda e: e.tensor_scalar(out=lg.t[:], in0=lg.t[:], scalar1=-1.0, scalar2=None, op0=ALU.mult), [lg.r], [lg.r])
    cst = TB(kb.sb([128, 8 * 128], F32))
    pc = TB(kb.sb([128, 8], F32))

    def tab(i):
        return tabs.t[:, i * 128:(i + 1) * 128]

    def ct(i):
        return cst.t[:, i * 128:(i + 1) * 128]

    def expt(dst, src, d):
        P.act(lambda e: e.activation(out=dst, in_=src, func=AF.Exp, scale=lg.t[:, d:d + 1]), [tabs.r, lg.r, cols.r], [cst.r])

    expt(ct(5), tab(0), 0)
    expt(ct(6), tab(2), 1)
    P.dve(lambda e: e.tensor_tensor(out=ct(5), in0=ct(5), in1=tab(1), op=ALU.mult), [cst.r, tabs.r], [cst.r])
    P.dve(lambda e: e.tensor_tensor(out=ct(6), in0=ct(6), in1=tab(3), op=ALU.mult), [cst.r, tabs.r], [cst.r])
    P.dve(lambda e: e.tensor_tensor(out=ct(0), in0=ct(5), in1=ct(6), op=ALU.add), [cst.r], [cst.r])
    expt(ct(1), tab(4), 0)
    expt(ct(2), tab(5), 1)
    expt(ct(3), tab(6), 0)
    expt(ct(4), tab(7), 1)
    for dcol, scol, d in ((0, 0, 0), (1, 1, 1), (2, 2, 0), (3, 3, 1)):
        P.act(lambda e, dcol=dcol, scol=scol, d=d: e.activation(
            out=pc.t[:, dcol:dcol + 1], in_=cols.t[:, scol:scol + 1], func=AF.Exp, scale=lg.t[:, d:d + 1]),
            [cols.r, lg.r], [pc.r])
    for d in (0, 1):
        P.act(lambda e, d=d: e.activation(out=pc.t[:, 4 + d:5 + d], in_=lg.t[:, d:d + 1], func=AF.Exp, scale=128.0),
              [lg.r], [pc.r])
    DS = {0: 0, 1: 1}
    DCH = {0: 4, 1: 5}
    DC = {0: 1, 1: 2}

    S = [TB(kb.sb([128, 2, 512], F32)) for _ in range(2)]
    Sb = [TB(kb.sb([128, 2, 512], BF16)) for _ in range(2)]
    ps_att = Ring(kb, 2, psum=True)
    ps_o = Ring(kb, 2, psum=True)
    ps_u = Ring(kb, 4, psum=True)
    a_r = Ring(kb, 2, [128, 128], BF16)
    qs_r = Ring(kb, 2, [128, 2, 128], BF16)
    ks_r = Ring(kb, 2, [128, 256], BF16)
    o_r = Ring(kb, 2, [128, 512], F32)
    xc_r = Ring(kb, 2, [128, 512], F32)
    sq_r = Ring(kb, 2, [128, 512], F32)
    st_r = Ring(kb, 2, [128, 4], F32)
    on_r = Ring(kb, 3, [128, 512], BF16)
    ob_r = Ring(kb, 3, [128, 512], F32)
    qg_r = Ring(kb, 2, [128, 1024], BF16)
    kg_r = Ring(kb, 2, [128, 1024], BF16)
    ktg_r = Ring(kb, 2, [128, 1024], BF16)
    vg_r = Ring(kb, 2, [128, 2048], BF16)
    toks = []

    def norm_out(o, chunk):
        stt = st_r.next()
        P.dve(lambda e: e.reduce_sum(out=stt.t[:, 0:1], in_=o.t[:], axis=AX.X), [o.r], [stt.r])
        P.dve(lambda e: e.tensor_scalar(out=stt.t[:, 0:1], in0=stt.t[:, 0:1], scalar1=-1.0 / 512, scalar2=None,
                                        op0=ALU.mult), [stt.r], [stt.r])
        xc = xc_r.next()
        P.dve(lambda e: e.tensor_scalar(out=xc.t[:], in0=o.t[:], scalar1=stt.t[:, 0:1], scalar2=None, op0=ALU.add),
              [o.r, stt.r], [xc.r])
        sq = sq_r.next()
        P.act(lambda e: e.activation(out=sq.t[:], in_=xc.t[:], func=AF.Square), [xc.r], [sq.r])
        P.dve(lambda e: e.reduce_sum(out=stt.t[:, 1:2], in_=sq.t[:], axis=AX.X), [sq.r, stt.r], [stt.r])
        P.act(lambda e: e.activation(out=stt.t[:, 2:3], in_=stt.t[:, 1:2], func=AF.Sqrt, bias=epsb.t[:, 0:1],
                                     scale=1.0 / 512), [stt.r, epsb.r], [stt.r])
        P.dve(lambda e: e.reciprocal(out=stt.t[:, 3:4], in_=stt.t[:, 2:3]), [stt.r], [stt.r])
        ob = on_r.next()
        P.act(lambda e: e.activation(out=ob.t[:], in_=xc.t[:], func=AF.Copy, scale=stt.t[:, 3:4]),
              [xc.r, stt.r], [ob.r])
        toks.append(P.dma(on_out[chunk], ob.t[:], reads=[ob.r]))

    def state_update(d, ktile_ap, ktile_res, v_ap, v_res, scale_ap, first=False):
        ks = ks_r.next()
        P.dve(lambda e: e.tensor_scalar(out=ks.t[:], in0=ktile_ap, scalar1=scale_ap, scalar2=None, op0=ALU.mult),
              [ktile_res, pc.r], [ks.r])
        return ks

    def upd(d, ks_list, v_list, v_res, init):
        for dch in range(2):
            pu = ps_u.next()
            n = len(ks_list)
            for i in range(n):
                P.pe(lambda e, pu=pu, i=i, dch=dch: e.matmul(
                    pu.t[:, :], lhsT=ks_list[i].t[:, dch * 128:(dch + 1) * 128], rhs=v_list[i],
                    start=(i == 0), stop=(i == n - 1)), [ks_list[i].r, v_res], [pu.r])
            if init:
                P.dve(lambda e, pu=pu, dch=dch: e.tensor_copy(out=S[d].t[:, dch, :], in_=pu.t[:, :]), [pu.r], [S[d].r])
            else:
                P.dve(lambda e, pu=pu, dch=dch: e.scalar_tensor_tensor(
                    out=S[d].t[:, dch, :], in0=S[d].t[:, dch, :], scalar=pc.t[:, DCH[d]:DCH[d] + 1], in1=pu.t[:, :],
                    op0=ALU.mult, op1=ALU.add), [pu.r, S[d].r, pc.r], [S[d].r])
            P.pool(lambda e, dch=dch: e.tensor_copy(out=Sb[d].t[:, dch, :], in_=S[d].t[:, dch, :]), [S[d].r], [Sb[d].r])

    qc = TB(kb.sb([128, 512], BF16))
    kc = TB(kb.sb([128, 512], BF16))
    kct = TB(kb.sb([128, 512], BF16))
    vct = TB(kb.sb([128, 1024], BF16))
    for t_, src in ((qc, qc_in), (kc, kc_in), (kct, kct_in), (vct, vct_in)):
        P.dma(t_.t[:], src[:, :], writes=[t_.r])
    wcol = {0: (2, 0), 1: (1, 3)}
    for d in (0, 1):
        ksl = []
        for tt in range(2):
            ksl.append(state_update(d, kct.t[:, tt * 256:(tt + 1) * 256], kct.r, None, None,
                                    pc.t[:, wcol[d][tt]:wcol[d][tt] + 1]))
        upd(d, ksl, [vct.t[:, 0:512], vct.t[:, 512:1024]], vct.r, True)
    if with_ctx:
        for it in range(2):
            al = []
            for jt in range(2):
                pa = ps_att.next()
                for dch in range(2):
                    P.pe(lambda e, pa=pa, dch=dch, jt=jt, it=it: e.matmul(
                        pa.t[:, :128], lhsT=kc.t[:, dch * 256 + jt * 128:dch * 256 + (jt + 1) * 128],
                        rhs=qc.t[:, dch * 256 + it * 128:dch * 256 + (it + 1) * 128],
                        start=(dch == 0), stop=(dch == 1)), [kc.r, qc.r], [pa.r])
                a = a_r.next()
                dtab = ct(0) if it == jt else (ct(3) if it > jt else ct(4))
                P.dve(lambda e, a=a, pa=pa, dtab=dtab: e.tensor_tensor(out=a.t[:], in0=pa.t[:, :128], in1=dtab, op=ALU.mult),
                      [pa.r, cst.r], [a.r])
                al.append(a)
            po = ps_o.next()
            for jt in range(2):
                P.pe(lambda e, po=po, jt=jt: e.matmul(po.t[:, :], lhsT=al[jt].t[:], rhs=vct.t[:, jt * 512:(jt + 1) * 512],
                                                      start=(jt == 0), stop=(jt == 1)), [al[jt].r, vct.r], [po.r])
            o = o_r.next()
            P.act(lambda e, o=o, po=po: e.activation(out=o.t[:], in_=po.t[:, :], func=AF.Copy), [po.r], [o.r])
            norm_out(o, NCH + it)

    def load_group(g, need_kT):
        qg, ktg, vg = qg_r.next(), ktg_r.next(), vg_r.next()
        P.dma(qg.t[:], qT_in[g], writes=[qg.r])
        P.dma(ktg.t[:], kt_in[g], writes=[ktg.r])
        P.dma(vg.t[:], vt_in[g], writes=[vg.r])
        kg = None
        if need_kT:
            kg = kg_r.next()
            P.dma(kg.t[:], kT_in[g], writes=[kg.r])
        return qg, kg, ktg, vg

    def scaled_q(qg, cc, d):
        qs = qs_r.next()
        for dch in range(2):
            P.pool(lambda e, dch=dch: e.tensor_tensor(
                out=qs.t[:, dch, :], in0=qg.t[:, dch * 512 + cc * 128:dch * 512 + (cc + 1) * 128], in1=ct(DC[d]),
                op=ALU.mult), [qg.r, cst.r], [qs.r])
        return qs

    def bwd_chunk(g, cc, grp):
        qg, kg, ktg, vg = grp
        c = 4 * g + cc
        qs = scaled_q(qg, cc, 1)
        po = ps_o.next()
        for dch in range(2):
            P.pe(lambda e, dch=dch: e.matmul(po.t[:, :], lhsT=qs.t[:, dch, :], rhs=Sb[1].t[:, dch, :],
                                             start=(dch == 0), stop=(dch == 1)), [qs.r, Sb[1].r], [po.r])
        ob = ob_r.next()
        P.act(lambda e: e.activation(out=ob.t[:], in_=po.t[:, :], func=AF.Copy), [po.r], [ob.r])
        P.dma(obD[c], ob.t[:], reads=[ob.r], writes=[r_obD[c]])
        ks = state_update(1, ktg.t[:, cc * 256:(cc + 1) * 256], ktg.r, None, None, pc.t[:, DS[1]:DS[1] + 1])
        upd(1, [ks], [vg.t[:, cc * 512:(cc + 1) * 512]], vg.r, False)

    for g in range(NG - 1, -1, -1):
        grp = load_group(g, False)
        for cc in range(3, -1, -1):
            bwd_chunk(g, cc, grp)

    def fwd_chunk(g, cc, grp):
        qg, kg, ktg, vg = grp
        c = 4 * g + cc
        ob = ob_r.next()
        P.dma(ob.t[:], obD[c], reads=[r_obD[c]], writes=[ob.r])
        pa = ps_att.next()
        for dch in range(2):
            P.pe(lambda e, dch=dch: e.matmul(
                pa.t[:, :128], lhsT=kg.t[:, dch * 512 + cc * 128:dch * 512 + (cc + 1) * 128],
                rhs=qg.t[:, dch * 512 + cc * 128:dch * 512 + (cc + 1) * 128],
                start=(dch == 0), stop=(dch == 1)), [kg.r, qg.r], [pa.r])
        a = a_r.next()
        P.dve(lambda e: e.tensor_tensor(out=a.t[:], in0=pa.t[:, :128], in1=ct(0), op=ALU.mult), [pa.r, cst.r], [a.r])
        qs = scaled_q(qg, cc, 0)
        po = ps_o.next()
        P.pe(lambda e: e.matmul(po.t[:, :], lhsT=a.t[:], rhs=vg.t[:, cc * 512:(cc + 1) * 512], start=True, stop=False),
             [a.r, vg.r], [po.r])
        for dch in range(2):
            P.pe(lambda e, dch=dch: e.matmul(po.t[:, :], lhsT=qs.t[:, dch, :], rhs=Sb[0].t[:, dch, :],
                                             start=False, stop=(dch == 1)), [qs.r, Sb[0].r], [po.r])
        o = o_r.next()
        P.dve(lambda e: e.tensor_tensor(out=o.t[:], in0=po.t[:, :], in1=ob.t[:], op=ALU.add), [po.r, ob.r], [o.r])
        norm_out(o, c)
        ks = state_update(0, ktg.t[:, cc * 256:(cc + 1) * 256], ktg.r, None, None, pc.t[:, DS[0]:DS[0] + 1])
        upd(0, [ks], [vg.t[:, cc * 512:(cc + 1) * 512]], vg.r, False)

    for g in range(NG):
        grp = load_group(g, True)
        for cc in range(4):
            fwd_chunk(g, cc, grp)
    return kb.finish(toks[-N_DMA_SEMS:])


NCORES = 8
TL, TCX = 4096, 64
SEQ, NCTX, DM = 16384, 256, 1024
_LAUNCHES = []


import os
_TIMES = []


def _run(nc, in_maps):
    if os.environ.get("K_TRACE"):
        res = run_bass_kernel_spmd(nc, in_maps, core_ids=list(range(NCORES)), trace=True)
        _TIMES.append(res.exec_time_ns)
        print("LAUNCH exec_ns", res.exec_time_ns, flush=True)
    else:
        res = run_bass_kernel_spmd(nc, in_maps, core_ids=list(range(NCORES)))
    return res.results


def _cvpack(c, cc):
    a_ = np.stack([c, cc], 0).reshape(2, 8, 128)
    return np.ascontiguousarray(a_.transpose(2, 1, 0).reshape(128, 16)).astype(np.float32)


def _adab_pack(b, m0, nch):
    return np.ascontiguousarray(b[m0 * 128:(m0 + nch) * 128].reshape(nch, 128).T).astype(np.float32)


def _gather_batch(outs, name, b):
    lat = np.concatenate([outs[b * 4 + r][name][:, :TL] for r in range(4)], 1)
    ctx = np.concatenate([outs[b * 4 + r][name][:, TL:] for r in range(4)], 1)
    return lat, ctx


def _vpack(v_tok, dv, aug=False):
    nk = v_tok.shape[0]
    return np.ascontiguousarray(v_tok.reshape(nk // 128, 128, dv).transpose(1, 0, 2).reshape(128, -1))


def na_bias_table(rel_bias):
    out = np.full((18, 16, 128, 256), -30000.0, np.float32)
    kk = np.arange(128)
    qq = np.arange(256)
    for var, j in ((0, 0), (1, 1), (2, 63)):
        for t in range(6):
            kr = (4 * j - 4 + 2 * t + kk // 64)[:, None]
            kc = (kk % 64)[:, None]
            qrow = (4 * j + qq // 64)[None, :]
            qc = (qq % 64)[None, :]
            r0 = np.clip(qrow - 4, 0, 248)
            c0 = np.clip(qc - 8, 0, 48)
            valid = (kr >= 0) & (kr <= 255) & (kr >= r0) & (kr < r0 + 8) & (kc >= c0) & (kc < c0 + 16)
            dr = np.clip(kr - qrow + 7, 0, 14)
            dc = np.clip(kc - qc + 15, 0, 30)
            vals = rel_bias[:, dr, dc]
            out[var * 6 + t] = np.where(valid[None], vals, np.float32(-30000.0))
    return out


def _pre_common(i, xs, inp):
    ims = []
    for c in range(NCORES):
        b = c // 4
        ims.append({"xT": xs[c], "mod": np.ascontiguousarray(inp["_mods"][i][b][:, 0:16, :]), "consts": host_consts()})
    return ims


def _post_common(i, xs, inp, T):
    ims = []
    for c in range(NCORES):
        b = c // 4
        ims.append({"xT": np.ascontiguousarray(xs[c][:, :T]),
                    "mod": np.ascontiguousarray(inp["_mods"][i][b][:, 16:48, :]), "consts": host_consts(),
                    "ffn_w_in": inp["ffn_w_in"][i], "ffn_w_out": inp["ffn_w_out"][i]})
    return ims


def layer_ret(xs, inp):
    ims = _pre_common(0, xs, inp)
    for c in range(NCORES):
        C, S = rope_tables(64, [['r'] * 128, ['c'] * 128], (c % 4) * TL, TL)
        ims[c].update({"w_in": inp["ret_w_in"][0], "ropeC": C, "ropeS": S})
    pre = _run(build_pre(0, TL, TCX), ims)
    ims = []
    for b in range(2):
        Q, Qc = _gather_batch(pre, "qT", b)
        K, Kc = _gather_batch(pre, "kT", b)
        V, Vc = _gather_batch(pre, "vT", b)
        for h in range(4):
            ims.append(ret_host_inputs(Q[h * 256:(h + 1) * 256], K[h * 256:(h + 1) * 256], V[h * 512:(h + 1) * 512],
                                       Qc[h * 256:(h + 1) * 256], Kc[h * 256:(h + 1) * 256], Vc[h * 512:(h + 1) * 512],
                                       inp["ret_decay_logit"][0][:, h]))
    mix = _run(build_ret(32), ims)
    ims = _post_common(0, xs, inp, TL + TCX)
    ng = np.ascontiguousarray(inp["ret_norm_g"][0].reshape(16, 128).T).astype(np.float32)
    for c in range(NCORES):
        b, r = c // 4, c % 4
        aT = np.zeros((2048, TL + TCX), NBF)
        for h in range(4):
            on = mix[b * 4 + h]["on"].reshape(130 * 128, 512)
            aT[h * 512:(h + 1) * 512, :TL] = on[r * TL:(r + 1) * TL].T
            aT[h * 512:(h + 1) * 512, TL:] = on[SEQ + r * TCX:SEQ + (r + 1) * TCX].T
        ims[c].update({"aT": aT, "sgT": pre[c]["sgT"], "ng": ng, "w_out": inp["ret_w_out"][0]})
    post = _run(build_post(0, TL, TCX), ims)
    return [post[c]["xoT"] for c in range(NCORES)]


def ret_host_inputs(qT, kT, vT, qcT, kcT, vcT, dl2):
    N = qT.shape[1]
    NG = N // 512

    def fm(a):
        return np.ascontiguousarray(a.reshape(2, 128, NG, 512).transpose(2, 1, 0, 3).reshape(NG, 128, 1024))

    def tm(a, F):
        return np.ascontiguousarray(a.T.reshape(NG, 4, 128, F).transpose(0, 2, 1, 3).reshape(NG, 128, 4 * F))

    def fmc(a):
        return np.ascontiguousarray(a.reshape(2, 128, 256).transpose(1, 0, 2).reshape(128, 512))

    def tmc(a, F):
        return np.ascontiguousarray(a.T.reshape(2, 128, F).transpose(1, 0, 2).reshape(128, 2 * F))

    tabs, cols = ret_tables()
    return {"qT": fm(qT), "kT": fm(kT), "ktok": tm(kT, 256), "vtok": tm(vT, 512), "qcT": fmc(qcT), "kcT": fmc(kcT),
            "kctok": tmc(kcT, 256), "vctok": tmc(vcT, 512),
            "dl": np.ascontiguousarray(np.broadcast_to(np.asarray(dl2, np.float32)[None, :], (128, 2))),
            "tabs": tabs, "cols": cols}


def layer_diff(xs, inp):
    i = 1
    import math
    lam_init = 0.8 - 0.6 * math.exp(-0.3 * i)
    ims = _pre_common(i, xs, inp)
    gains = np.ascontiguousarray(np.stack([np.tile(inp["diff_q_norm_g"][0], 2), np.tile(inp["diff_k_norm_g"][0], 2)], 1))
    for c in range(NCORES):
        C, S = rope_tables(16, [(['r'] * 32 + ['c'] * 32) * 2], (c % 4) * TL, TL)
        ims[c].update({"w_in": inp["diff_w_in"][0], "ropeC": C, "ropeS": S, "gains": gains.astype(np.float32)})
    pre = _run(build_pre(1, TL, TCX), ims)
    NK = SEQ + NCTX
    kv = []
    for b in range(2):
        K, Kc = _gather_batch(pre, "kT", b)
        V, Vc = _gather_batch(pre, "vT", b)
        kT = np.ascontiguousarray(np.concatenate([Kc, K], 1).reshape(8, 128, NK))
        Vall = np.concatenate([Vc, V], 1)
        v = np.stack([_vpack(np.ascontiguousarray(Vall[h * 128:(h + 1) * 128].T), 128) for h in range(8)], 0)
        kv.append((kT, v))
    ims = []
    for c in range(NCORES):
        b = c // 4
        ims.append({"qT": np.ascontiguousarray(pre[c]["qT"].reshape(8, 128, TL + TCX)), "kT": kv[b][0], "v": kv[b][1],
                    "lam": np.ascontiguousarray(inp["diff_lambda"][0]).astype(np.float32)})
    allk = [(t, None) for t in range(NK // 128)]
    qtiles = [(q0, 512, allk) for q0 in range(0, TL, 512)] + [(TL, TCX, [(0, None), (1, None)])]
    mix = _run(build_att(8, 128, TL + TCX, NK, 128, 2, qtiles, 64 ** -0.5, lam_init=lam_init, out_dt=BF16), ims)
    ims = _post_common(i, xs, inp, TL + TCX)
    for c in range(NCORES):
        ims[c].update({"aT": np.ascontiguousarray(mix[c]["oT"].reshape(1024, TL + TCX)),
                       "sg": np.ascontiguousarray(inp["diff_subln_g"][0].reshape(128, 1)).astype(np.float32),
                       "w_out": inp["diff_w_out"][0]})
    post = _run(build_post(1, TL, TCX, lam_init), ims)
    return [post[c]["xoT"] for c in range(NCORES)]


def layer_na(xs, inp):
    i = 2
    ims = _pre_common(i, xs, inp)
    gains = np.ascontiguousarray(np.stack([np.tile(inp["na_q_norm_g"][0], 2), np.tile(inp["na_k_norm_g"][0], 2)], 1))
    for c in range(NCORES):
        ims[c].update({"w_in": inp["na_w_in"][0], "gains": gains.astype(np.float32)})
    pre = _run(build_pre(2, TL, TCX), ims)
    bias0 = na_bias_table(np.asarray(inp["na_rel_bias"][0], np.float32))
    NKL = 72 * 64
    NK = NCTX + NKL
    ims = []
    for b in range(2):
        K, Kc = _gather_batch(pre, "kT", b)
        V, Vc = _gather_batch(pre, "vT", b)
        for r in range(4):
            rows = np.clip(np.arange(64 * r - 4, 64 * r + 68), 0, 255)
            tok = (rows[:, None] * 64 + np.arange(64)[None, :]).reshape(-1)
            kT = np.ascontiguousarray(np.concatenate([Kc, K[:, tok]], 1).reshape(16, 64, NK))
            Vall = np.concatenate([Vc, V[:, tok]], 1)
            v = np.stack([_vpack(np.ascontiguousarray(Vall[h * 64:(h + 1) * 64].T), 64, True) for h in range(16)], 0)
            bias = bias0.copy()
            if r != 0:
                bias[0:6] = bias0[6:12]
            if r != 3:
                bias[12:18] = bias0[6:12]
            ims.append({"qT": np.ascontiguousarray(pre[b * 4 + r]["qT"].reshape(16, 64, TL + TCX)), "kT": kT, "v": v,
                        "bias": bias})
    qtiles = []
    for jj in range(16):
        slot = 0 if jj == 0 else (2 if jj == 15 else 1)
        ktl = [(0, None), (1, None)] + [(2 + 2 * jj + t, slot * 6 + t) for t in range(6)]
        qtiles.append((jj * 256, 256, ktl))
    qtiles.append((TL, TCX, [(0, None), (1, None)]))
    mix = _run(build_att(16, 64, TL + TCX, NK, 64, 1, qtiles, 64 ** -0.5, bias_shape=[18, 16, 128, 256], aug=True, out_dt=BF16), ims)
    ims = _post_common(i, xs, inp, TL + TCX)
    for c in range(NCORES):
        ims[c].update({"aT": np.ascontiguousarray(mix[c]["oT"].reshape(1024, TL + TCX)), "w_out": inp["na_w_out"][0]})
    post = _run(build_post(2, TL, TCX), ims)
    return [post[c]["xoT"] for c in range(NCORES)]


def layer_mla(xs, inp):
    i = 3
    ims = _pre_common(i, xs, inp)
    hw = mla_host_weights(inp["mla_w_down"][0], inp["mla_w_uq"][0], inp["mla_w_ukv"][0], inp["mla_q_norm_g"][0],
                          inp["mla_kv_norm_g"][0], inp["mla_qk_norm_q"][0], inp["mla_qk_norm_k"][0])
    for c in range(NCORES):
        C, S = rope_tables(8, [(['r'] * 16 + ['c'] * 16) * 4], (c % 4) * TL, TL)
        ims[c].update(hw)
        ims[c].update({"ropeC": C, "ropeS": S})
    pre = _run(build_pre(3, TL, TCX), ims)
    NK = SEQ + NCTX
    kv = []
    for b in range(2):
        Kn, Knc = _gather_batch(pre, "knT", b)
        Kr, Krc = _gather_batch(pre, "krT", b)
        V, Vc = _gather_batch(pre, "vT", b)
        kT = np.ascontiguousarray(np.concatenate([Knc, Kn], 1).reshape(16, 64, NK))
        KrA = np.ascontiguousarray(np.concatenate([Krc, Kr], 1))
        Vall = np.concatenate([Vc, V], 1)
        v = np.stack([_vpack(np.ascontiguousarray(Vall[h * 64:(h + 1) * 64].T), 64, True) for h in range(16)], 0)
        kv.append((kT, v, KrA))
    ims = []
    for c in range(NCORES):
        b = c // 4
        qn = pre[c]["qnT"][:, :TL].reshape(16, 64, TL)
        qr = pre[c]["qrT"][:, :TL].reshape(16, 32, TL)
        ims.append({"qT": np.ascontiguousarray(np.concatenate([qn, qr], 1)), "kT": kv[b][0], "v": kv[b][1],
                    "kS": kv[b][2]})
    allk = [(t, None) for t in range(NK // 128)]
    qtiles = [(q0, 512, allk) for q0 in range(0, TL, 512)]
    mix = _run(build_att(16, 96, TL, NK, 64, 1, qtiles, 96 ** -0.5, aug=True, out_dt=BF16, kshared=32), ims)
    ims = _post_common(i, xs, inp, TL)
    for c in range(NCORES):
        ims[c].update({"aT": np.ascontiguousarray(mix[c]["oT"].reshape(1024, TL)), "w_out": inp["mla_w_out"][0]})
    post = _run(build_post(3, TL, 0), ims)
    return [post[c]["xoT"] for c in range(NCORES)]


def _split_state(x_lat, x_ctx):
    xs = []
    for c in range(NCORES):
        b, r = c // 4, c % 4
        xs.append(np.ascontiguousarray(np.concatenate([x_lat[b, r * TL:(r + 1) * TL].T,
                                                       x_ctx[b, r * TCX:(r + 1) * TCX].T], 1)).astype(np.float32))
    return xs


def _join_state(xs):
    x_lat = np.zeros((2, SEQ, DM), np.float32)
    x_ctx = np.zeros((2, NCTX, DM), np.float32)
    for c in range(NCORES):
        b, r = c // 4, c % 4
        x_lat[b, r * TL:(r + 1) * TL] = xs[c][:, :TL].T
        if xs[c].shape[1] > TL:
            x_ctx[b, r * TCX:(r + 1) * TCX] = xs[c][:, TL:].T
    return x_lat, x_ctx


def kernel(**inputs):
    inp = {k: np.asarray(v) for k, v in inputs.items()}
    xs = _split_state(inp["x"].astype(np.float32), inp["ctx"].astype(np.float32))
    inp["_mods"] = compute_mods(inp)
    for fn in (layer_ret, layer_diff, layer_na, layer_mla):
        xs = fn(xs, inp)
    return _join_state(xs)[0]


def build_mod():
    kb = KB()
    P = kb.P
    cv_in = kb.din("cv3", [128, 24], F32)
    aw_in = kb.din("aw", [4, 1024, 768], F32)
    ab_in = kb.din("ab", [128, 24], F32)
    out = kb.dout("modo", [128, 24, 3], F32)
    cv = TB(kb.sb([128, 24], F32))
    ab = TB(kb.sb([128, 24], F32))
    mod = TB(kb.sb([128, 24, 3], F32))
    P.dma(cv.t[:], cv_in[:, :], writes=[cv.r])
    P.dma(ab.t[:], ab_in[:, :], writes=[ab.r])
    P.act(lambda e: e.activation(out=cv.t[:], in_=cv.t[:], func=AF.Silu), [cv.r], [cv.r])
    blk = Ring(kb, 2, [128, 8, 768], F32)
    psr = Ring(kb, 4, psum=True)
    for l in range(4):
        b = blk.next()
        P.dma(b.t[:], aw_in[l].rearrange("(k p) c -> p k c", p=128), writes=[b.r])
        for m in range(6):
            ps = psr.next()
            for k in range(8):
                P.pe(lambda e, ps=ps, b=b, k=k, m=m: e.matmul(
                    ps.t[:, 0:3], lhsT=b.t[:, k, m * 128:(m + 1) * 128], rhs=cv.t[:, 3 * k:3 * k + 3],
                    start=(k == 0), stop=(k == 7)), [b.r, cv.r], [ps.r])
            P.dve(lambda e, ps=ps, l=l, m=m: e.tensor_scalar(
                out=mod.t[:, l * 6 + m, :], in0=ps.t[:, 0:3], scalar1=ab.t[:, l * 6 + m:l * 6 + m + 1], scalar2=None,
                op0=ALU.add), [ps.r, ab.r], [mod.r])
    t = P.dma(out[:, :, :], mod.t[:], reads=[mod.r])
    return kb.finish([t])


def compute_mods(inp):
    c3 = np.stack([inp["c"][0], inp["c"][1], inp["c_ctx"]], 0).reshape(3, 8, 128)
    cv3 = np.ascontiguousarray(c3.transpose(2, 1, 0).reshape(128, 24)).astype(np.float32)
    ims = []
    for c in range(NCORES):
        aw = np.ascontiguousarray(inp["ada_w"][:, :, c * 768:(c + 1) * 768]).astype(np.float32)
        ab = inp["ada_b"][:, c * 768:(c + 1) * 768].reshape(4, 6, 128)
        ims.append({"cv3": cv3, "aw": aw, "ab": np.ascontiguousarray(ab.transpose(2, 0, 1).reshape(128, 24)).astype(np.float32)})
    res = _run(build_mod(), ims)
    full = np.zeros((4, 128, 48, 3), np.float32)
    for c in range(NCORES):
        o = res[c]["modo"].reshape(128, 4, 6, 3)
        for l in range(4):
            full[l][:, c * 6:(c + 1) * 6, :] = o[:, l]
    mods = []
    for l in range(4):
        mods.append([np.ascontiguousarray(full[l][:, :, [b, 2]]) for b in range(2)])
    return mods
```
